# Optimizing a Trainium2 kernel written in Bass

```python
import math
import jax, jax.numpy as jnp
from jax import lax
import numpy as np

D_MODEL = 2048
BATCH = 1
SEQ = 8192
DEPTH = 2

N_MIXERS = 2
N_ATTN_LAYERS = (DEPTH + 1) // 2
N_SSM_LAYERS = DEPTH // 2
HEAD_DIM = 128
N_Q_HEADS = D_MODEL // HEAD_DIM
N_KV_HEADS = N_Q_HEADS // 4
GQA_GROUP = N_Q_HEADS // N_KV_HEADS
ATTN_WIDTH = N_Q_HEADS * HEAD_DIM
KV_WIDTH = N_KV_HEADS * HEAD_DIM
ATTN_IN_WIDTH = 2 * ATTN_WIDTH + 2 * KV_WIDTH
Q_BLOCK = 128
ROPE_THETA = 10000.0
ROPE_AXIS_DIM = HEAD_DIM // 2
GRID_W = 64
QK_EPS = 1e-6
SSM_WIDTH = D_MODEL
SSM_GROUP = 16
N_SSM_GROUPS = SSM_WIDTH // SSM_GROUP
STATE_DIM = 64
N_DIRS = 2
DT_MIN = 1e-3
DT_MAX = 1e-1
LN_EPS = 1e-5
DEEPNORM_ALPHA = (2.0 * DEPTH) ** 0.25
DEEPNORM_BETA = (8.0 * DEPTH) ** -0.25

kernel_name = "hybrid_gqa_axialrope_s5_deepnorm_encoder"


def layer_norm(x, g, b):
    xf = x.astype(jnp.float32)
    mu = jnp.mean(xf, axis=-1, keepdims=True)
    xc = xf - mu
    var = jnp.mean(xc * xc, axis=-1, keepdims=True)
    y = xc * lax.rsqrt(var + LN_EPS) * g.astype(jnp.float32) + b.astype(jnp.float32)
    return y.astype(x.dtype)


def rms_norm_heads(x, g):
    xf = x.astype(jnp.float32)
    ms = jnp.mean(xf * xf, axis=-1, keepdims=True)
    return xf * lax.rsqrt(ms + QK_EPS) * g.astype(jnp.float32)


def axial_rope_tables(seq_len):
    rows = seq_len // GRID_W
    row_id = jnp.repeat(jnp.arange(rows, dtype=jnp.float32), GRID_W)
    col_id = jnp.tile(jnp.arange(GRID_W, dtype=jnp.float32), rows)
    inv_freq = ROPE_THETA ** (-jnp.arange(0, ROPE_AXIS_DIM, 2, dtype=jnp.float32) / ROPE_AXIS_DIM)
    ang_r = row_id[:, None] * inv_freq[None, :]
    ang_c = col_id[:, None] * inv_freq[None, :]
    return jnp.cos(ang_r), jnp.sin(ang_r), jnp.cos(ang_c), jnp.sin(ang_c)


def _rotate_half(x, cos, sin):
    x1, x2 = jnp.split(x, 2, axis=-1)
    c = cos[None, :, None, :]
    s = sin[None, :, None, :]
    return jnp.concatenate([x1 * c - x2 * s, x2 * c + x1 * s], axis=-1)


def apply_axial_rope(x, tables):
    cos_r, sin_r, cos_c, sin_c = tables
    x_row, x_col = jnp.split(x, 2, axis=-1)
    return jnp.concatenate([_rotate_half(x_row, cos_r, sin_r), _rotate_half(x_col, cos_c, sin_c)], axis=-1)


def attention_mixer(h, tables, w_in, q_g, k_g, w_out):
    b, s, _ = h.shape
    proj = jnp.einsum('bsd,de->bse', h, w_in)
    q, k, v, z = jnp.split(proj, [ATTN_WIDTH, ATTN_WIDTH + KV_WIDTH, ATTN_WIDTH + 2 * KV_WIDTH], axis=-1)
    q = apply_axial_rope(rms_norm_heads(q.reshape(b, s, N_Q_HEADS, HEAD_DIM), q_g), tables)
    k = apply_axial_rope(rms_norm_heads(k.reshape(b, s, N_KV_HEADS, HEAD_DIM), k_g), tables)
    v = v.reshape(b, s, N_KV_HEADS, HEAD_DIM).astype(jnp.float32)
    n_blocks = s // Q_BLOCK
    qb = q.reshape(b, n_blocks, Q_BLOCK, N_KV_HEADS, GQA_GROUP, HEAD_DIM).transpose(1, 0, 2, 3, 4, 5)
    scale = HEAD_DIM ** -0.5

    def one_block(q_blk):
        scores = jnp.einsum('bqhgd,bkhd->bhgqk', q_blk, k) * scale
        p = jax.nn.softmax(scores, axis=-1)
        return jnp.einsum('bhgqk,bkhd->bqhgd', p, v)

    o = lax.map(one_block, qb)
    o = o.transpose(1, 0, 2, 3, 4, 5).reshape(b, s, ATTN_WIDTH)
    o = o * jax.nn.silu(z.astype(jnp.float32))
    return jnp.einsum('bse,ed->bsd', o.astype(h.dtype), w_out)


def _cmul(ar, ai, br, bi):
    return ar * br - ai * bi, ar * bi + ai * br


def _ssm_combine(left, right):
    al_re, al_im, bl_re, bl_im = left
    ar_re, ar_im, br_re, br_im = right
    a_re, a_im = _cmul(ar_re, ar_im, al_re, al_im)
    s_re, s_im = _cmul(ar_re, ar_im, bl_re, bl_im)
    return a_re, a_im, s_re + br_re, s_im + br_im


def s5_direction(u, lam_re, lam_im, log_dt, b_re, b_im, c_re, c_im, reverse):
    f32 = jnp.float32
    lam_re = lam_re.astype(f32)
    lam_im = lam_im.astype(f32)
    dt = jnp.exp(log_dt.astype(f32))[:, None]
    mag = jnp.exp(lam_re * dt)
    ab_re = mag * jnp.cos(lam_im * dt)
    ab_im = mag * jnp.sin(lam_im * dt)
    den = lam_re * lam_re + lam_im * lam_im
    nr = ab_re - 1.0
    coef_re = (nr * lam_re + ab_im * lam_im) / den
    coef_im = (ab_im * lam_re - nr * lam_im) / den
    bb_re, bb_im = _cmul(coef_re[..., None], coef_im[..., None], b_re.astype(f32), b_im.astype(f32))
    bu_re = jnp.einsum('sbgc,gpc->sbgp', u, bb_re)
    bu_im = jnp.einsum('sbgc,gpc->sbgp', u, bb_im)
    a_shape = (u.shape[0], 1) + ab_re.shape
    a_re = jnp.broadcast_to(ab_re, a_shape)
    a_im = jnp.broadcast_to(ab_im, a_shape)
    _, _, s_re, s_im = lax.associative_scan(_ssm_combine, (a_re, a_im, bu_re, bu_im), reverse=reverse, axis=0)
    return jnp.einsum('gcp,sbgp->sbgc', c_re.astype(f32), s_re) - jnp.einsum('gcp,sbgp->sbgc', c_im.astype(f32), s_im)


def ssm_mixer(h, w_in, lam_re, lam_im, log_dt, b_re, b_im, c_re, c_im, d_skip, w_glu, b_glu, w_out):
    b, s, _ = h.shape
    f32 = jnp.float32
    proj = jnp.einsum('bsd,de->bse', h, w_in)
    u, z = jnp.split(proj, 2, axis=-1)
    uf = u.astype(f32)
    ug = uf.reshape(b, s, N_SSM_GROUPS, SSM_GROUP).transpose(1, 0, 2, 3)
    y_fwd = s5_direction(ug, lam_re[0], lam_im[0], log_dt[0], b_re[0], b_im[0], c_re[0], c_im[0], False)
    y_bwd = s5_direction(ug, lam_re[1], lam_im[1], log_dt[1], b_re[1], b_im[1], c_re[1], c_im[1], True)
    y = (y_fwd + y_bwd).transpose(1, 0, 2, 3).reshape(b, s, SSM_WIDTH) + d_skip.astype(f32) * uf
    g = jax.nn.gelu(y)
    y = g * jax.nn.sigmoid(jnp.einsum('bse,ef->bsf', g, w_glu.astype(f32)) + b_glu.astype(f32))
    y = y * jax.nn.silu(z.astype(f32))
    return jnp.einsum('bse,ed->bsd', y.astype(h.dtype), w_out)


def setup_inputs(seed: int = 0) -> dict:
    key = jax.random.key(seed)
    ks = jax.random.split(key, 20)
    f32 = jnp.float32
    nA, nS, G, P, C = N_ATTN_LAYERS, N_SSM_LAYERS, N_SSM_GROUPS, STATE_DIM, SSM_GROUP
    x = jax.random.normal(ks[0], (BATCH, SEQ, D_MODEL), f32)
    ln_g = 1.0 + 0.02 * jax.random.normal(ks[1], (DEPTH, D_MODEL), f32)
    ln_b = 0.02 * jax.random.normal(ks[2], (DEPTH, D_MODEL), f32)
    w_in_attn = jax.random.normal(ks[3], (nA, D_MODEL, ATTN_IN_WIDTH), f32) * D_MODEL ** -0.5
    q_norm_g = 1.0 + 0.02 * jax.random.normal(ks[4], (nA, HEAD_DIM), f32)
    k_norm_g = 1.0 + 0.02 * jax.random.normal(ks[5], (nA, HEAD_DIM), f32)
    w_out_attn = jax.random.normal(ks[6], (nA, ATTN_WIDTH, D_MODEL), f32) * (ATTN_WIDTH ** -0.5 * DEEPNORM_BETA)
    w_in_ssm = jax.random.normal(ks[7], (nS, D_MODEL, 2 * SSM_WIDTH), f32) * D_MODEL ** -0.5
    lam_re = -0.5 + 0.01 * jax.random.normal(ks[8], (nS, N_DIRS, G, P), f32)
    lam_im = math.pi * jnp.arange(P, dtype=f32) + 0.01 * jax.random.normal(ks[9], (nS, N_DIRS, G, P), f32)
    log_dt = jax.random.uniform(ks[10], (nS, N_DIRS, G), f32, math.log(DT_MIN), math.log(DT_MAX))
    b_scale = (2.0 * C) ** -0.5
    b_re = jax.random.normal(ks[11], (nS, N_DIRS, G, P, C), f32) * b_scale
    b_im = jax.random.normal(ks[12], (nS, N_DIRS, G, P, C), f32) * b_scale
    c_scale = 2.0 ** -0.5
    c_re = jax.random.normal(ks[13], (nS, N_DIRS, G, C, P), f32) * c_scale
    c_im = jax.random.normal(ks[14], (nS, N_DIRS, G, C, P), f32) * c_scale
    d_skip = jax.random.normal(ks[15], (nS, SSM_WIDTH), f32)
    w_glu = jax.random.normal(ks[16], (nS, SSM_WIDTH, SSM_WIDTH), f32) * SSM_WIDTH ** -0.5
    b_glu = 0.02 * jax.random.normal(ks[17], (nS, SSM_WIDTH), f32)
    w_out_ssm = jax.random.normal(ks[18], (nS, SSM_WIDTH, D_MODEL), f32) * (SSM_WIDTH ** -0.5 * DEEPNORM_BETA)
    return {"x": x, "ln_g": ln_g, "ln_b": ln_b,
            "w_in_attn": w_in_attn, "q_norm_g": q_norm_g, "k_norm_g": k_norm_g, "w_out_attn": w_out_attn,
            "w_in_ssm": w_in_ssm, "lam_re": lam_re, "lam_im": lam_im, "log_dt": log_dt,
            "b_re": b_re, "b_im": b_im, "c_re": c_re, "c_im": c_im, "d_skip": d_skip,
            "w_glu": w_glu, "b_glu": b_glu, "w_out_ssm": w_out_ssm}


def reference(x, ln_g, ln_b, w_in_attn, q_norm_g, k_norm_g, w_out_attn, w_in_ssm, lam_re, lam_im, log_dt,
              b_re, b_im, c_re, c_im, d_skip, w_glu, b_glu, w_out_ssm):
    tables = axial_rope_tables(x.shape[1])
    h = x
    for i in range(DEPTH):
        j = i // N_MIXERS
        if i % N_MIXERS == 0:
            out = attention_mixer(h, tables, w_in_attn[j], q_norm_g[j], k_norm_g[j], w_out_attn[j])
        else:
            out = ssm_mixer(h, w_in_ssm[j], lam_re[j], lam_im[j], log_dt[j], b_re[j], b_im[j], c_re[j], c_im[j],
                            d_skip[j], w_glu[j], b_glu[j], w_out_ssm[j])
        h = layer_norm(DEEPNORM_ALPHA * h + out.astype(h.dtype), ln_g[i], ln_b[i])
    return h
```

```python
import math
import ml_dtypes
import numpy as np
from contextlib import ExitStack
import concourse.bass as bass
import concourse.mybir as mybir
from concourse.bass_utils import run_bass_kernel_spmd

F32 = mybir.dt.float32
BF16 = mybir.dt.bfloat16
AF = mybir.ActivationFunctionType
ALU = mybir.AluOpType
AX = mybir.AxisListType
ENGS = ("pe", "act", "dve", "pool", "sp")


class Res:
    def __init__(self, name):
        self.name = name
        self.w = None
        self.r = {}


class Ctx:
    def __init__(self, nc, es, plan=None):
        self.nc = nc
        self.es = es
        self.plan = plan
        self.emit = plan is not None
        self.eng = {"pe": nc.tensor, "act": nc.scalar, "dve": nc.vector, "pool": nc.gpsimd, "sp": nc.sync}
        self.idx = {e: 0 for e in ENGS}
        self.seen = {e: {} for e in ENGS}
        self.record = set()
        self.dcount = {}
        self.dsem = {}
        self.esem = {}
        self.rank = {}
        if self.emit:
            for e in ENGS:
                self.esem[e] = es.enter_context(nc.semaphore("es_" + e))
                ids = sorted(i for (k, i) in plan if k == e)
                self.rank[e] = {i: n + 1 for n, i in enumerate(ids)}

    def _wait(self, e, key, val):
        if key == e and e == "pe":
            return
        if self.seen[e].get(key, -1) >= val:
            return
        self.seen[e][key] = val
        if key in ENGS:
            self.record.add((key, val))
            if self.emit:
                self.eng[e].wait_ge(self.esem[key], self.rank[key][val])
        else:
            if self.emit:
                self.eng[e].wait_ge(self.dsem[key], val)

    def _deps(self, e, reads, writes):
        for r in reads:
            if r.w is not None:
                self._wait(e, *r.w)
        for w in writes:
            if w.w is not None:
                self._wait(e, *w.w)
            for k, v in w.r.items():
                self._wait(e, k, v)

    def _mark(self, ev, reads, writes):
        for r in reads:
            r.r[ev[0]] = ev[1]
        for w in writes:
            w.w = ev
            w.r = {}

    def op(self, e, fn, reads=(), writes=()):
        self._deps(e, reads, writes)
        i = self.idx[e]
        self.idx[e] = i + 1
        if self.emit:
            ins = fn()
            if (e, i) in self.plan:
                ins.then_inc(self.esem[e], 1)
        self.seen[e][e] = max(self.seen[e].get(e, -1), -1)
        self._mark((e, i), reads, writes)

    def dma(self, q, out, in_, reads=(), writes=(), key=None, **kw):
        self._deps(q, reads, writes)
        if key is None:
            key = "d_" + (writes[0].name if writes else reads[0].name)
        if key not in self.dcount:
            self.dcount[key] = 0
            if self.emit:
                self.dsem[key] = self.es.enter_context(self.nc.semaphore(key))
        self.dcount[key] += 16
        if self.emit:
            self.eng[q].dma_start(out=out(), in_=in_(), **kw).then_inc(self.dsem[key], 16)
        self._mark((key, self.dcount[key]), reads, writes)

    def barrier(self):
        last = {e: self.idx[e] - 1 for e in ENGS if self.idx[e] > 0}
        for e in ENGS:
            for k, v in last.items():
                if k != e:
                    self._wait(e, k, v)
            for k, v in self.dcount.items():
                self._wait(e, k, v)

    def finish(self, e="sp"):
        for k, v in self.dcount.items():
            self._wait(e, k, v)


def build2(build_fn):
    nc1 = bass.Bass("TRN2", target_bir_lowering=False)
    with ExitStack() as es:
        c1 = Ctx(nc1, es, plan=None)
        build_fn(nc1, c1)
        plan = set(c1.record)
    nc2 = bass.Bass("TRN2", target_bir_lowering=False)
    with ExitStack() as es:
        c2 = Ctx(nc2, es, plan=plan)
        build_fn(nc2, c2)
    return nc2

S_LOC = 1024; DM = 2048; NT = 8
ATT_IN = 5120
QK_EPS = 1e-6

def build_proj(nc, cx, mode):
    es = cx.es
    ssm = (mode == 'ssm')
    WCOLS = 4096 if ssm else ATT_IN
    NCB = WCOLS // 512
    def dram(name, shape, dt, kind):
        return nc.dram_tensor(name, shape, dt, kind=kind).ap()
    xT = dram("xT", [DM, S_LOC], F32, "ExternalInput")
    w_in = dram("w_in", [DM, WCOLS], F32, "ExternalInput")
    gq = dram("gq", [1, 128], F32, "ExternalInput")
    gk = dram("gk", [1, 128], F32, "ExternalInput")
    ctab = dram("ctab", [S_LOC, 128], F32, "ExternalInput")
    sntab = dram("sntab", [S_LOC, 64], F32, "ExternalInput")
    sptab = dram("sptab", [S_LOC, 64], F32, "ExternalInput")
    if ssm:
        u_o = dram("u_o", [S_LOC, 2048], F32, "ExternalOutput")
    else:
        q_o = dram("q_o", [S_LOC, 2048], BF16, "ExternalOutput")
        k_o = dram("k_o", [S_LOC, 512], BF16, "ExternalOutput")
        v_o = dram("v_o", [S_LOC, 512], BF16, "ExternalOutput")
    gz_o = dram("gz_o", [S_LOC, 2048], F32, "ExternalOutput")

    def sb(name, shape, dt):
        return es.enter_context(nc.sbuf_tensor(name, shape, dt))
    def ps(name, shape, dt=F32):
        return es.enter_context(nc.psum_tensor(name, shape, dt))

    xTb = sb("xTb", [128, 16, S_LOC], BF16); r_x = [Res("xTb%d" % c) for c in range(16)]
    wb = [sb("wb%d" % i, [128, 16, 512], BF16) for i in range(2)]; r_w = [Res("wb%d" % i) for i in range(2)]
    gqt = sb("gqt", [128, 128], F32); r_gq = Res("gqt")
    gkt = sb("gkt", [128, 128], F32); r_gk = Res("gkt")
    ct = sb("ct", [128, NT, 128], F32); snt = sb("snt", [128, NT, 64], F32); spt = sb("spt", [128, NT, 64], F32)
    r_tab = Res("tab")
    pss = [ps("ps%d" % i, [128, 512]) for i in range(4)]; r_ps = [Res("ps%d" % i) for i in range(4)]
    NB = 3
    ss = [sb("ss%d" % i, [128, 4], F32) for i in range(NB)]; r_ss = [Res("ss%d" % i) for i in range(NB)]
    junk = sb("junk", [128, 128], F32); r_junk = Res("junk")
    epst = sb("epst", [128, 1], F32)
    r_eps = Res("epst")
    cx.op("dve", lambda: nc.vector.memset(epst[:], QK_EPS), writes=[r_eps])
    qn = [sb("qn%d" % i, [128, 128], F32) for i in range(NB)]; r_qn = [Res("qn%d" % i) for i in range(NB)]
    t1 = [sb("t1_%d" % i, [128, 128], F32) for i in range(NB)]; r_t1 = [Res("t1_%d" % i) for i in range(NB)]
    t2 = [sb("t2_%d" % i, [128, 128], F32) for i in range(NB)]; r_t2 = [Res("t2_%d" % i) for i in range(NB)]
    ob = [sb("ob%d" % i, [128, 512], BF16) for i in range(NB)]; r_ob = [Res("ob%d" % i) for i in range(NB)]
    of = [sb("of%d" % i, [128, 512], F32) for i in range(NB)]; r_of = [Res("of%d" % i) for i in range(NB)]

    for c in range(16):
        cx.dma("pool", lambda c=c: xTb[:, c, :], lambda c=c: xT[c * 128:(c + 1) * 128, :], writes=[r_x[c]], key="d_x")
    cx.dma("sp", lambda: gqt[:], lambda: gq[0:1, :].to_broadcast([128, 128]), writes=[r_gq], key="d_c")
    cx.dma("sp", lambda: gkt[:], lambda: gk[0:1, :].to_broadcast([128, 128]), writes=[r_gk], key="d_c")
    cx.dma("sp", lambda: ct[:], lambda: ctab.rearrange("(t p) n -> p t n", p=128), writes=[r_tab], key="d_c")
    cx.dma("sp", lambda: snt[:], lambda: sntab.rearrange("(t p) n -> p t n", p=128), writes=[r_tab], key="d_c")
    cx.dma("sp", lambda: spt[:], lambda: sptab.rearrange("(t p) n -> p t n", p=128), writes=[r_tab], key="d_c")

    w_view = w_in.rearrange("(c p) n -> p c n", p=128)
    def load_w(cb):
        b = cb % 2
        cx.dma("pool", lambda: wb[b][:], lambda: w_view[:, :, cb * 512:(cb + 1) * 512], writes=[r_w[b]], key="d_w%d" % b)
    load_w(0)
    it = 0
    for cb in range(NCB):
        if cb + 1 < NCB:
            load_w(cb + 1)
        b = cb % 2
        for t in range(NT):
            pi = it % 4; bi = it % NB; it += 1
            P = pss[pi]
            for c in range(16):
                cx.op("pe", lambda c=c, P=P, t=t, b=b: nc.tensor.matmul(P[:], lhsT=xTb[:, c, t * 128:(t + 1) * 128], rhs=wb[b][:, c, :], start=(c == 0), stop=(c == 15)),
                      reads=[r_x[c], r_w[b]], writes=[r_ps[pi]])
            if ssm and cb < 4:
                cx.op("act", lambda P=P, bi=bi: nc.scalar.copy(out=of[bi][:], in_=P[:]), reads=[r_ps[pi]], writes=[r_of[bi]])
                cx.dma("sp", lambda t=t, cb=cb: u_o[t * 128:(t + 1) * 128, cb * 512:(cb + 1) * 512], lambda bi=bi: of[bi][:], reads=[r_of[bi]], key="d_of%d" % bi)
            elif ssm:
                cx.op("act", lambda P=P, bi=bi: nc.scalar.activation(out=of[bi][:], in_=P[:], func=AF.Silu), reads=[r_ps[pi]], writes=[r_of[bi]])
                cx.dma("sp", lambda t=t, cb=cb: gz_o[t * 128:(t + 1) * 128, (cb - 4) * 512:(cb - 3) * 512], lambda bi=bi: of[bi][:], reads=[r_of[bi]], key="d_of%d" % bi)
            elif cb <= 4:
                g_t, r_g = (gqt, r_gq) if cb < 4 else (gkt, r_gk)
                for h in range(4):
                    cx.op("act", lambda h=h, P=P, bi=bi: nc.scalar.activation(out=junk[:], in_=P[:, h * 128:(h + 1) * 128], func=AF.Square, accum_out=ss[bi][:, h:h + 1]),
                          reads=[r_ps[pi]], writes=[r_junk, r_ss[bi]])
                cx.op("act", lambda bi=bi: nc.scalar.activation(out=ss[bi][:], in_=ss[bi][:], func=AF.Sqrt, scale=1.0 / 128, bias=epst[:, 0:1]),
                      reads=[r_ss[bi], r_eps], writes=[r_ss[bi]])
                cx.op("dve", lambda bi=bi: nc.vector.reciprocal(out=ss[bi][:], in_=ss[bi][:]),
                      reads=[r_ss[bi]], writes=[r_ss[bi]])
                for h in range(4):
                    hb = (it * 4 + h) % NB
                    cx.op("dve", lambda h=h, P=P, bi=bi, hb=hb, g_t=g_t: nc.vector.scalar_tensor_tensor(out=qn[hb][:], in0=P[:, h * 128:(h + 1) * 128], scalar=ss[bi][:, h:h + 1], in1=g_t[:], op0=ALU.mult, op1=ALU.mult),
                          reads=[r_ps[pi], r_ss[bi], r_g], writes=[r_qn[hb]])
                    cx.op("pool", lambda hb=hb, t=t: nc.gpsimd.tensor_tensor(out=t1[hb][:], in0=qn[hb][:], in1=ct[:, t, :], op=ALU.mult),
                          reads=[r_qn[hb], r_tab], writes=[r_t1[hb]])
                    def v4(ap):
                        return ap.rearrange("p (a h m) -> p a h m", a=2, h=2)
                    def v3(ap):
                        return ap.rearrange("p (a m) -> p a m", a=2)
                    cx.op("dve", lambda hb=hb, t=t: nc.vector.tensor_tensor(out=v4(t2[hb][:])[:, :, 0, :], in0=v4(qn[hb][:])[:, :, 1, :], in1=v3(snt[:, t, :]), op=ALU.mult),
                          reads=[r_qn[hb], r_tab], writes=[r_t2[hb]])
                    cx.op("dve", lambda hb=hb, t=t: nc.vector.tensor_tensor(out=v4(t2[hb][:])[:, :, 1, :], in0=v4(qn[hb][:])[:, :, 0, :], in1=v3(spt[:, t, :]), op=ALU.mult),
                          reads=[r_qn[hb], r_tab], writes=[r_t2[hb]])
                    cx.op("dve", lambda hb=hb, h=h, bi=bi: nc.vector.tensor_tensor(out=ob[bi][:, h * 128:(h + 1) * 128], in0=t1[hb][:], in1=t2[hb][:], op=ALU.add),
                          reads=[r_t1[hb], r_t2[hb]], writes=[r_ob[bi]])
                dst = q_o[t * 128:(t + 1) * 128, cb * 512:(cb + 1) * 512] if cb < 4 else k_o[t * 128:(t + 1) * 128, :]
                cx.dma("sp", lambda dst=dst: dst, lambda bi=bi: ob[bi][:], reads=[r_ob[bi]], key="d_ob%d" % bi)
            elif cb == 5:
                cx.op("act", lambda P=P, bi=bi: nc.scalar.copy(out=ob[bi][:], in_=P[:]), reads=[r_ps[pi]], writes=[r_ob[bi]])
                cx.dma("sp", lambda t=t: v_o[t * 128:(t + 1) * 128, :], lambda bi=bi: ob[bi][:], reads=[r_ob[bi]], key="d_ob%d" % bi)
            else:
                cx.op("act", lambda P=P, bi=bi: nc.scalar.activation(out=of[bi][:], in_=P[:], func=AF.Silu), reads=[r_ps[pi]], writes=[r_of[bi]])
                cx.dma("sp", lambda t=t, cb=cb: gz_o[t * 128:(t + 1) * 128, (cb - 6) * 512:(cb - 5) * 512], lambda bi=bi: of[bi][:], reads=[r_of[bi]], key="d_of%d" % bi)
    cx.finish("sp")


def rope_tables(pos):
    row = (pos // 64).astype(np.float32); col = (pos % 64).astype(np.float32)
    inv = (10000.0 ** (-np.arange(0, 64, 2, dtype=np.float32) / 64)).astype(np.float32)
    ar = row[:, None] * inv[None, :]; ac = col[:, None] * inv[None, :]
    cr, sr, cc, sc = np.cos(ar), np.sin(ar), np.cos(ac), np.sin(ac)
    C = np.concatenate([cr, cr, cc, cc], axis=1).astype(np.float32)
    Sn = np.concatenate([-sr, -sc], axis=1).astype(np.float32)
    Sp = np.concatenate([sr, sc], axis=1).astype(np.float32)
    return C, Sn, Sp


LN_EPS = 1e-5
ALPHA = (2.0 * 2) ** 0.25
SCALE = 128 ** -0.5

def outproj_ln(nc, cx, es, srcT, r_srcT, w_dram, x_dram, g_dram, b_dram, out_dram, ps_list, r_ps_list):
    def sb(name, shape, dt):
        return es.enter_context(nc.sbuf_tensor(name, shape, dt))
    wo = sb("wo", [128, 16, 2048], BF16); r_wo = [Res("wo%d" % i) for i in range(4)]
    gt = sb("lng", [128, 2048], F32); bt = sb("lnb", [128, 2048], F32); r_gb = Res("lngb")
    xt = [sb("xres%d" % i, [128, 2048], F32) for i in range(2)]; r_xt = [Res("xres%d" % i) for i in range(2)]
    rt = [sb("rres%d" % i, [128, 2048], F32) for i in range(2)]; r_rt = [Res("rres%d" % i) for i in range(2)]
    st6 = [sb("st6_%d" % i, [128, 4, 6], F32) for i in range(2)]; r_st6 = [Res("st6_%d" % i) for i in range(2)]
    mv = [sb("mv%d" % i, [128, 4], F32) for i in range(2)]; r_mv = [Res("mv%d" % i) for i in range(2)]
    epsl = sb("epsl", [128, 1], F32); r_epsl = Res("epsl")
    cx.op("dve", lambda: nc.vector.memset(epsl[:], LN_EPS), writes=[r_epsl])
    wv = w_dram.rearrange("(c p) n -> p c n", p=128)
    for cb in range(4):
        cx.dma("pool", lambda cb=cb: wo[:, :, cb * 512:(cb + 1) * 512], lambda cb=cb: wv[:, :, cb * 512:(cb + 1) * 512], writes=[r_wo[cb]], key="d_wo")
    cx.dma("sp", lambda: gt[:], lambda: g_dram[0:1, :].to_broadcast([128, 2048]), writes=[r_gb], key="d_c2")
    cx.dma("sp", lambda: bt[:], lambda: b_dram[0:1, :].to_broadcast([128, 2048]), writes=[r_gb], key="d_c2")
    for st in range(8):
        b = st % 2
        cx.dma("sp", lambda st=st, b=b: xt[b][:], lambda st=st: x_dram[st * 128:(st + 1) * 128, :], writes=[r_xt[b]], key="d_xr%d" % b)
        for cb in range(4):
            P = ps_list[cb]
            for c in range(16):
                cx.op("pe", lambda c=c, P=P, st=st, cb=cb: nc.tensor.matmul(P[:], lhsT=srcT[:, c, st * 128:(st + 1) * 128], rhs=wo[:, c, cb * 512:(cb + 1) * 512], start=(c == 0), stop=(c == 15)),
                      reads=[r_srcT[c], r_wo[cb]], writes=[r_ps_list[cb]])
            cx.op("dve", lambda P=P, b=b, cb=cb: nc.vector.scalar_tensor_tensor(out=rt[b][:, cb * 512:(cb + 1) * 512], in0=xt[b][:, cb * 512:(cb + 1) * 512], scalar=ALPHA, in1=P[:], op0=ALU.mult, op1=ALU.add),
                  reads=[r_xt[b], r_ps_list[cb]], writes=[r_rt[b]])
            cx.op("dve", lambda b=b, cb=cb: nc.vector.bn_stats(out=st6[b][:, cb, :], in_=rt[b][:, cb * 512:(cb + 1) * 512]), reads=[r_rt[b]], writes=[r_st6[b]])
        cx.op("dve", lambda b=b: nc.vector.bn_aggr(out=mv[b][:, 0:2], in_=st6[b][:].rearrange("p a s -> p (a s)")), reads=[r_st6[b]], writes=[r_mv[b]])
        cx.op("act", lambda b=b: nc.scalar.activation(out=mv[b][:, 2:3], in_=mv[b][:, 1:2], func=AF.Sqrt, bias=epsl[:, 0:1], scale=1.0), reads=[r_mv[b], r_epsl], writes=[r_mv[b]])
        cx.op("dve", lambda b=b: nc.vector.reciprocal(out=mv[b][:, 2:3], in_=mv[b][:, 2:3]), reads=[r_mv[b]], writes=[r_mv[b]])
        cx.op("dve", lambda b=b: nc.vector.tensor_scalar(out=mv[b][:, 3:4], in0=mv[b][:, 0:1], scalar1=mv[b][:, 2:3], scalar2=-1.0, op0=ALU.mult, op1=ALU.mult), reads=[r_mv[b]], writes=[r_mv[b]])
        cx.op("act", lambda b=b: nc.scalar.activation(out=xt[b][:], in_=rt[b][:], func=AF.Identity, scale=mv[b][:, 2:3], bias=mv[b][:, 3:4]), reads=[r_rt[b], r_mv[b]], writes=[r_xt[b]])
        cx.op("dve", lambda b=b: nc.vector.tensor_tensor(out=rt[b][:], in0=xt[b][:], in1=gt[:], op=ALU.mult), reads=[r_xt[b], r_gb], writes=[r_rt[b]])
        cx.op("pool", lambda b=b: nc.gpsimd.tensor_tensor(out=xt[b][:], in0=rt[b][:], in1=bt[:], op=ALU.add), reads=[r_rt[b], r_gb], writes=[r_xt[b]])
        cx.dma("sp", lambda st=st: out_dram[st * 128:(st + 1) * 128, :], lambda b=b: xt[b][:], reads=[r_xt[b]], key="d_xo%d" % b)


def transposes_tok2feat(nc, cx, og, r_og, ogT, r_ogT, identb, r_id, psT, r_psT):
    k = 0
    for st in range(8):
        for c4 in range(4):
            pi = k % 2; k += 1
            Tv = psT[pi][:].bitcast(BF16)
            for j in range(4):
                c = c4 * 4 + j
                cx.op("pe", lambda Tv=Tv, j=j, st=st, c=c: nc.tensor.transpose(out=Tv[:, j * 128:(j + 1) * 128], in_=og[:, st, c * 128:(c + 1) * 128], identity=identb[:]),
                      reads=[r_og[st], r_id], writes=[r_psT[pi]])
            if k % 2:
                cx.op("act", lambda Tv=Tv, c4=c4, st=st: nc.scalar.copy(out=ogT[:, c4 * 4:(c4 + 1) * 4, st * 128:(st + 1) * 128], in_=Tv[:, 0:512].rearrange("p (j t) -> p j t", j=4)),
                      reads=[r_psT[pi]], writes=[r_ogT[c4 * 4 + j] for j in range(4)])
            else:
                cx.op("dve", lambda Tv=Tv, c4=c4, st=st: nc.vector.tensor_copy(out=ogT[:, c4 * 4:(c4 + 1) * 4, st * 128:(st + 1) * 128], in_=Tv[:, 0:512].rearrange("p (j t) -> p j t", j=4)),
                      reads=[r_psT[pi]], writes=[r_ogT[c4 * 4 + j] for j in range(4)])


def build_B(nc, cx):
    def dram(name, shape, dt, kind):
        return nc.dram_tensor(name, shape, dt, kind=kind).ap()
    qT = dram("qT", [16, 128, S_LOC], BF16, "ExternalInput")
    KT = dram("KT", [4, 128, 8192], BF16, "ExternalInput")
    V = dram("V", [4, 8192, 128], BF16, "ExternalInput")
    gz = dram("gz", [S_LOC, 2048], F32, "ExternalInput")
    x = dram("x", [S_LOC, 2048], F32, "ExternalInput")
    w_out = dram("w_out", [2048, 2048], F32, "ExternalInput")
    ln_g = dram("ln_g", [1, 2048], F32, "ExternalInput")
    ln_b = dram("ln_b", [1, 2048], F32, "ExternalInput")
    ident = dram("ident", [128, 128], F32, "ExternalInput")
    h1 = dram("h1", [S_LOC, 2048], F32, "ExternalOutput")

    with ExitStack() as es0:
        def sb0(name, shape, dt):
            return es0.enter_context(nc.sbuf_tensor(name, shape, dt))
        def ps0(name, shape, dt=F32):
            return es0.enter_context(nc.psum_tensor(name, shape, dt))
        og = sb0("og", [128, 8, 2048], BF16); r_og = [Res("og%d" % i) for i in range(8)]
        identb = sb0("identb", [128, 128], BF16); r_id = Res("identb")
        cx.dma("pool", lambda: identb[:], lambda: ident[:, :], writes=[r_id], key="d_c")
        sbuf_ps = [ps0("sps%d" % i, [128, 1024]) for i in range(2)]; r_s = [Res("sps%d" % i) for i in range(2)]
        O = [ps0("ops%d" % i, [128, 512]) for i in range(4)]; r_O = [Res("ops%d" % i) for i in range(4)]
        with ExitStack() as es1:
            def sb(name, shape, dt):
                return es1.enter_context(nc.sbuf_tensor(name, shape, dt))
            qTs = sb("qTs", [128, 16, S_LOC], BF16); r_q = [Res("qTs%d" % i) for i in range(16)]
            KTs = [sb("KTs%d" % i, [128, 8192], BF16) for i in range(2)]; r_K = [Res("KTs%d" % i) for i in range(2)]
            Vs = [sb("Vs%d" % i, [128, 64, 129], BF16) for i in range(2)]; r_V = [Res("Vs%d" % i) for i in range(2)]
            PT = [sb("PT%d" % i, [128, 1024], BF16) for i in range(3)]; r_PT = [Res("PT%d" % i) for i in range(3)]
            gzt = [sb("gzt%d" % i, [128, 4, 128], F32) for i in range(2)]; r_gz = [Res("gzt%d" % i) for i in range(2)]
            rinv = [sb("rinv%d" % i, [128, 4], F32) for i in range(2)]; r_ri = [Res("rinv%d" % i) for i in range(2)]
            for i in range(2):
                cx.op("pool", lambda i=i: nc.gpsimd.memset(Vs[i][:, :, 128:129], 1.0), writes=[r_V[i]])
            def load_kv(h):
                b = h % 2
                for half in range(2):
                    cx.dma("sp", lambda: KTs[b][:, half * 4096:(half + 1) * 4096], lambda: KT[h, :, half * 4096:(half + 1) * 4096], writes=[r_K[b]], key="d_K%d" % b)
                    cx.dma("sp", lambda: Vs[b][:, half * 32:(half + 1) * 32, 0:128], lambda: V[h, half * 4096:(half + 1) * 4096, :].rearrange("(t p) d -> p t d", p=128), writes=[r_V[b]], key="d_V%d" % b)
            load_kv(0)
            for hd in range(16):
                cx.dma("act", lambda hd=hd: qTs[:, hd, :], lambda hd=hd: qT[hd, :, :], writes=[r_q[hd]], key="d_q")
            iters = [(kvh, g, qh, kp) for kvh in range(4) for g in range(4) for qh in range(2) for kp in range(32)]
            NI = len(iters)

            def emit_qk(n):
                kvh, g, qh, kp = iters[n]
                b = kvh % 2; head = kvh * 4 + g; si = n % 2
                if kp == 0:
                    gb = (n // 32) % 2
                    cx.dma("act", lambda gb=gb, head=head, qh=qh: gzt[gb][:], lambda head=head, qh=qh: gz[qh * 512:(qh + 1) * 512, head * 128:(head + 1) * 128].rearrange("(s p) e -> p s e", p=128), writes=[r_gz[gb]], key="d_gz%d" % gb)
                S = sbuf_ps[si]
                for j2 in range(2):
                    kt = kp * 2 + j2
                    cx.op("pe", lambda S=S, j2=j2, kt=kt, b=b, head=head, qh=qh: nc.tensor.matmul(S[:, j2 * 512:(j2 + 1) * 512], lhsT=KTs[b][:, kt * 128:(kt + 1) * 128], rhs=qTs[:, head, qh * 512:(qh + 1) * 512], start=True, stop=True),
                          reads=[r_K[b], r_q[head]], writes=[r_s[si]])

            def emit_rest(n):
                kvh, g, qh, kp = iters[n]
                b = kvh % 2; head = kvh * 4 + g; si = n % 2; pi = n % 3
                gb = (n // 32) % 2
                S = sbuf_ps[si]
                cx.op("act", lambda S=S, pi=pi: nc.scalar.activation(out=PT[pi][:], in_=S[:], func=AF.Exp, scale=SCALE), reads=[r_s[si]], writes=[r_PT[pi]])
                for j2 in range(2):
                    kt = kp * 2 + j2
                    for sub in range(4):
                        cx.op("pe", lambda sub=sub, pi=pi, j2=j2, kt=kt, b=b, kp=kp: nc.tensor.matmul(O[sub][:, 0:129], lhsT=PT[pi][:, j2 * 512 + sub * 128:j2 * 512 + (sub + 1) * 128], rhs=Vs[b][:, kt, :], start=(kp == 0 and j2 == 0), stop=(kp == 31 and j2 == 1)),
                              reads=[r_PT[pi], r_V[b]], writes=[r_O[sub]])
                if kp == 31:
                    for sub in range(4):
                        cx.op("dve", lambda sub=sub, gb=gb: nc.vector.reciprocal(out=rinv[gb][:, sub:sub + 1], in_=O[sub][:, 128:129]), reads=[r_O[sub]], writes=[r_ri[gb]])
                        cx.op("dve", lambda sub=sub, gb=gb, qh=qh, head=head: nc.vector.scalar_tensor_tensor(out=og[:, qh * 4 + sub, head * 128:(head + 1) * 128], in0=O[sub][:, 0:128], scalar=rinv[gb][:, sub:sub + 1], in1=gzt[gb][:, sub, :], op0=ALU.mult, op1=ALU.mult),
                              reads=[r_O[sub], r_ri[gb], r_gz[gb]], writes=[r_og[qh * 4 + sub]])
                if g == 0 and qh == 0 and kp == 1 and kvh + 1 < 4:
                    load_kv(kvh + 1)

            emit_qk(0)
            for n in range(NI):
                if n + 1 < NI:
                    emit_qk(n + 1)
                emit_rest(n)
        cx.barrier()
        with ExitStack() as es2:
            def sb(name, shape, dt):
                return es2.enter_context(nc.sbuf_tensor(name, shape, dt))
            ogT = sb("ogT", [128, 16, S_LOC], BF16); r_ogT = [Res("ogT%d" % i) for i in range(16)]
            transposes_tok2feat(nc, cx, og, r_og, ogT, r_ogT, identb, r_id, sbuf_ps, r_s)
            outproj_ln(nc, cx, es2, ogT, r_ogT, w_out, x, ln_g, ln_b, h1, O, r_O)
    cx.finish("sp")


NG = 16; NJ = 1024; NSEG = 16; SEGL = 64
SPB = 512 // SEGL
PI = math.pi

def build_D(nc, cx):
    def dram(name, shape, dt, kind):
        return nc.dram_tensor(name, shape, dt, kind=kind).ap()
    U = dram("U", [NG, 128, NJ], F32, "ExternalInput")
    lamre = dram("lamre", [128, NG], F32, "ExternalInput")
    lamim = dram("lamim", [128, NG], F32, "ExternalInput")
    logdt = dram("logdt", [128, NG], F32, "ExternalInput")
    Bre = dram("Bre", [128, NG, 16], F32, "ExternalInput")
    Bim = dram("Bim", [128, NG, 16], F32, "ExternalInput")
    Cre = dram("Cre", [128, NG, 16], F32, "ExternalInput")
    Cim = dram("Cim", [128, NG, 16], F32, "ExternalInput")
    dsk = dram("dsk", [128, NG], F32, "ExternalInput")
    maskF = dram("maskF", [128, 128], F32, "ExternalInput")
    maskB = dram("maskB", [128, 128], F32, "ExternalInput")
    ident = dram("ident", [128, 128], F32, "ExternalInput")
    Gy = dram("Gy", [NG, 128, NJ], F32, "ExternalOutput")

    with ExitStack() as es0:
        def sb0(name, shape, dt=F32):
            return es0.enter_context(nc.sbuf_tensor(name, shape, dt))
        def ps0(name, shape, dt=F32):
            return es0.enter_context(nc.psum_tensor(name, shape, dt))
        PS = [ps0("ps%d" % i, [128, 512]) for i in range(8)]; r_PS = [Res("ps%d" % i) for i in range(8)]
        W1T = sb0("W1T", [128, NG, 2, 128], BF16); r_W1T = Res("W1T")
        Tm = sb0("Tm", [128, NG, 128], BF16); r_Tm = Res("Tm")
        W3 = sb0("W3", [128, NG, 2, 128], BF16); r_W3 = Res("W3")
        a8 = sb0("a8", [128, 2, NG]); r_a8 = Res("a8")
        dskt = sb0("dskt", [128, NG]); r_dsk = Res("dskt")
        cx.dma("sp", lambda: dskt[:], lambda: dsk[:, :], writes=[r_dsk], key="d_c")

        with ExitStack() as es1:
            cnt = [0]
            def sb(shape, dt=F32):
                cnt[0] += 1
                return es1.enter_context(nc.sbuf_tensor("pp%d" % cnt[0], shape, dt))
            R = Res("prep")
            def dv(fn, eng="dve"):
                cx.op(eng, fn, reads=[R], writes=[R])
            lr = sb([128, NG]); li = sb([128, NG]); ld = sb([128, NG])
            bre = sb([128, NG, 16]); bim = sb([128, NG, 16]); cre = sb([128, NG, 16]); cim = sb([128, NG, 16])
            mF = sb([128, 128]); mB = sb([128, 128]); idf = sb([128, 128])
            for dst, src in [(lr, lamre), (li, lamim), (ld, logdt), (mF, maskF), (mB, maskB), (idf, ident)]:
                cx.dma("sp", lambda dst=dst: dst[:], lambda src=src: src[:, :], writes=[R], key="d_c")
            for dst, src in [(bre, Bre), (bim, Bim), (cre, Cre), (cim, Cim)]:
                cx.dma("sp", lambda dst=dst: dst[:], lambda src=src: src[:, :, :], writes=[R], key="d_c")
            dt_ = sb([128, NG]); er = sb([128, NG]); th = sb([128, NG]); mag = sb([128, NG])
            halfpi = sb([128, 1])
            dv(lambda: nc.vector.memset(halfpi[:], PI / 2))
            dv(lambda: nc.scalar.activation(out=dt_[:], in_=ld[:], func=AF.Exp), "act")
            dv(lambda: nc.vector.tensor_tensor(out=er[:], in0=lr[:], in1=dt_[:], op=ALU.mult))
            dv(lambda: nc.vector.tensor_tensor(out=th[:], in0=li[:], in1=dt_[:], op=ALU.mult))
            dv(lambda: nc.scalar.activation(out=mag[:], in_=er[:], func=AF.Exp), "act")
            ure = sb([128, NG]); uim = sb([128, NG]); t1 = sb([128, NG]); t2 = sb([128, NG])
            dv(lambda: nc.scalar.activation(out=uim[:], in_=th[:], func=AF.Sin, scale=0.125), "act")
            dv(lambda: nc.scalar.activation(out=ure[:], in_=th[:], func=AF.Sin, scale=-0.125, bias=halfpi[:, 0:1]), "act")
            for _ in range(3):
                dv(lambda: nc.vector.tensor_tensor(out=t1[:], in0=ure[:], in1=ure[:], op=ALU.mult))
                dv(lambda: nc.vector.tensor_tensor(out=t2[:], in0=uim[:], in1=uim[:], op=ALU.mult))
                dv(lambda: nc.vector.scalar_tensor_tensor(out=uim[:], in0=ure[:], scalar=2.0, in1=uim[:], op0=ALU.mult, op1=ALU.mult))
                dv(lambda: nc.vector.tensor_tensor(out=ure[:], in0=t1[:], in1=t2[:], op=ALU.subtract))
            are = sb([128, NG]); aim = sb([128, NG]); nre = sb([128, NG]); nim = sb([128, NG])
            dv(lambda: nc.vector.tensor_tensor(out=are[:], in0=ure[:], in1=mag[:], op=ALU.mult))
            dv(lambda: nc.vector.tensor_tensor(out=aim[:], in0=uim[:], in1=mag[:], op=ALU.mult))
            rmag = sb([128, NG])
            dv(lambda: nc.vector.reciprocal(out=rmag[:], in_=mag[:]))
            dv(lambda: nc.vector.tensor_tensor(out=nre[:], in0=ure[:], in1=rmag[:], op=ALU.mult))
            dv(lambda: nc.vector.scalar_tensor_tensor(out=nim[:], in0=uim[:], scalar=-1.0, in1=rmag[:], op0=ALU.mult, op1=ALU.mult))
            den = sb([128, NG]); nr = sb([128, NG]); cfr = sb([128, NG]); cfi = sb([128, NG])
            dv(lambda: nc.vector.tensor_tensor(out=den[:], in0=lr[:], in1=lr[:], op=ALU.mult))
            dv(lambda: nc.vector.tensor_tensor(out=t1[:], in0=li[:], in1=li[:], op=ALU.mult))
            dv(lambda: nc.vector.tensor_tensor(out=den[:], in0=den[:], in1=t1[:], op=ALU.add))
            dv(lambda: nc.vector.reciprocal(out=den[:], in_=den[:]))
            dv(lambda: nc.vector.tensor_scalar(out=nr[:], in0=are[:], scalar1=-1.0, scalar2=None, op0=ALU.add))
            dv(lambda: nc.vector.tensor_tensor(out=t1[:], in0=nr[:], in1=lr[:], op=ALU.mult))
            dv(lambda: nc.vector.tensor_tensor(out=t2[:], in0=aim[:], in1=li[:], op=ALU.mult))
            dv(lambda: nc.vector.tensor_tensor(out=t1[:], in0=t1[:], in1=t2[:], op=ALU.add))
            dv(lambda: nc.vector.tensor_tensor(out=cfr[:], in0=t1[:], in1=den[:], op=ALU.mult))
            dv(lambda: nc.vector.tensor_tensor(out=t1[:], in0=aim[:], in1=lr[:], op=ALU.mult))
            dv(lambda: nc.vector.tensor_tensor(out=t2[:], in0=nr[:], in1=li[:], op=ALU.mult))
            dv(lambda: nc.vector.tensor_tensor(out=t1[:], in0=t1[:], in1=t2[:], op=ALU.subtract))
            dv(lambda: nc.vector.tensor_tensor(out=cfi[:], in0=t1[:], in1=den[:], op=ALU.mult))

            def cmul(ore, oim, xre, xim, yre, yim, ta, tb):
                dv(lambda: nc.vector.tensor_tensor(out=ta(), in0=xre(), in1=yre(), op=ALU.mult))
                dv(lambda: nc.vector.tensor_tensor(out=tb(), in0=xim(), in1=yim(), op=ALU.mult))
                dv(lambda: nc.vector.tensor_tensor(out=ta(), in0=ta(), in1=tb(), op=ALU.subtract))
                dv(lambda: nc.vector.tensor_tensor(out=tb(), in0=xre(), in1=yim(), op=ALU.mult))
                dv(lambda: nc.vector.tensor_tensor(out=oim(), in0=xim(), in1=yre(), op=ALU.mult))
                dv(lambda: nc.vector.tensor_tensor(out=oim(), in0=oim(), in1=tb(), op=ALU.add))
                dv(lambda: nc.vector.tensor_copy(out=ore(), in_=ta()))
            bbr = sb([128, NG, 16]); bbi = sb([128, NG, 16]); ta3 = sb([128, NG, 16]); tb3 = sb([128, NG, 16])
            bc = lambda t: t[:, :].unsqueeze(2).to_broadcast([128, NG, 16])
            cmul(lambda: bbr[:], lambda: bbi[:], lambda: bre[:], lambda: bim[:], lambda: bc(cfr), lambda: bc(cfi), lambda: ta3[:], lambda: tb3[:])
            pw = sb([128, 2, NG, 9]); pn = sb([128, 2, NG, 9])
            tk = sb([128, NG, 4]); tk2 = sb([128, NG, 4])
            for tab, xr, xi in [(pw, are, aim), (pn, nre, nim)]:
                dv(lambda tab=tab: nc.vector.memset(tab[:, 0, :, 0:1], 1.0))
                dv(lambda tab=tab: nc.vector.memset(tab[:, 1, :, 0:1], 0.0))
                dv(lambda tab=tab, xr=xr: nc.vector.tensor_copy(out=tab[:, 0, :, 1], in_=xr[:]))
                dv(lambda tab=tab, xi=xi: nc.vector.tensor_copy(out=tab[:, 1, :, 1], in_=xi[:]))
                for (lo, n) in [(1, 1), (2, 2), (4, 4)]:
                    n_eff = min(n, 9 - (lo + 1))
                    cmul(lambda tab=tab, lo=lo, n_eff=n_eff: tab[:, 0, :, lo + 1:lo + 1 + n_eff], lambda tab=tab, lo=lo, n_eff=n_eff: tab[:, 1, :, lo + 1:lo + 1 + n_eff],
                         lambda tab=tab, n_eff=n_eff: tab[:, 0, :, 1:1 + n_eff], lambda tab=tab, n_eff=n_eff: tab[:, 1, :, 1:1 + n_eff],
                         lambda tab=tab, lo=lo, n_eff=n_eff: tab[:, 0, :, lo:lo + 1].to_broadcast([128, NG, n_eff]), lambda tab=tab, lo=lo, n_eff=n_eff: tab[:, 1, :, lo:lo + 1].to_broadcast([128, NG, n_eff]),
                         lambda n_eff=n_eff: tk[:, :, 0:n_eff], lambda n_eff=n_eff: tk2[:, :, 0:n_eff])
            dv(lambda: nc.vector.tensor_copy(out=a8[:, 0, :], in_=pw[:, 0, :, 8]))
            dv(lambda: nc.vector.tensor_copy(out=a8[:, 1, :], in_=pw[:, 1, :, 8]))
            cx.op("dve", lambda: nc.vector.tensor_copy(out=a8[:, 0, :], in_=pw[:, 0, :, 8]), reads=[R], writes=[r_a8])
            cx.op("dve", lambda: nc.vector.tensor_copy(out=a8[:, 1, :], in_=pw[:, 1, :, 8]), reads=[R], writes=[r_a8])
            E = sb([128, 2, NG, 8]); En = sb([128, 2, NG, 8]); Fo = sb([128, 2, NG, 8])
            for s in range(8):
                for ri in range(2):
                    dv(lambda s=s, ri=ri: nc.vector.tensor_copy(out=E[0:64, ri, :, s], in_=pw[0:64, ri, :, 7 - s]))
                    dv(lambda s=s, ri=ri: nc.vector.tensor_copy(out=E[64:128, ri, :, s], in_=pw[64:128, ri, :, s]))
                    dv(lambda s=s, ri=ri: nc.vector.tensor_copy(out=En[0:64, ri, :, s], in_=pn[0:64, ri, :, 7 - s]))
                    dv(lambda s=s, ri=ri: nc.vector.tensor_copy(out=En[64:128, ri, :, s], in_=pn[64:128, ri, :, s]))
                    dv(lambda s=s, ri=ri: nc.vector.tensor_copy(out=Fo[0:64, ri, :, s], in_=pw[0:64, ri, :, s + 1]))
                    dv(lambda s=s, ri=ri: nc.vector.tensor_copy(out=Fo[64:128, ri, :, s], in_=pw[64:128, ri, :, 8 - s]))
            big = [128, NG, 8, 16]
            W1f = sb([128, 2, NG, 128]); W3n = sb([128, 2, NG, 128]); W3f = sb([128, 2, NG, 128])
            tA = sb([128, NG, 128]); tB = sb([128, NG, 128])
            v4 = lambda t, ri: t[:, ri, :, :].rearrange("p g (s c) -> p g s c", c=16)
            t4 = lambda t: t[:].rearrange("p g (s c) -> p g s c", c=16)
            eb = lambda t, ri: t[:, ri, :, :].unsqueeze(3).to_broadcast(big)
            pb = lambda t: t[:, :, :].unsqueeze(2).to_broadcast(big)
            for (dstt, tab, pr, pi_) in [(W1f, E, bbr, bbi), (W3n, En, cre, cim), (W3f, Fo, cre, cim)]:
                cmul(lambda dstt=dstt: v4(dstt, 0), lambda dstt=dstt: v4(dstt, 1), lambda tab=tab: eb(tab, 0), lambda tab=tab: eb(tab, 1),
                     lambda pr=pr: pb(pr), lambda pi_=pi_: pb(pi_), lambda: t4(tA), lambda: t4(tB))
            dv(lambda: nc.vector.tensor_scalar(out=W3n[:, 1, :, :], in0=W3n[:, 1, :, :], scalar1=-1.0, scalar2=None, op0=ALU.mult))
            dv(lambda: nc.vector.tensor_scalar(out=W3f[:, 1, :, :], in0=W3f[:, 1, :, :], scalar1=-1.0, scalar2=None, op0=ALU.mult))
            cx.op("dve", lambda: nc.vector.tensor_copy(out=W3[:, :, 0, :], in_=W3f[:, 0, :, :]), reads=[R], writes=[r_W3])
            cx.op("dve", lambda: nc.vector.tensor_copy(out=W3[:, :, 1, :], in_=W3f[:, 1, :, :]), reads=[R], writes=[r_W3])
            tmpT = sb([128, 128])
            r_tmpT = Res("tmpT")
            for g in range(NG):
                for ri in range(2):
                    pi = (g * 2 + ri) % 4
                    cx.op("pe", lambda g=g, ri=ri, pi=pi: nc.tensor.transpose(out=PS[pi][:, 0:128], in_=W1f[:, ri, g, :], identity=idf[:]), reads=[R], writes=[r_PS[pi]])
                    cx.op("act", lambda g=g, ri=ri, pi=pi: nc.scalar.copy(out=W1T[:, g, ri, :], in_=PS[pi][:, 0:128]), reads=[r_PS[pi]], writes=[r_W1T])
                pf = 4 + (g % 2) * 2; pb_ = pf + 1
                for (pp, lo) in [(pf, 0), (pb_, 64)]:
                    for ri in range(2):
                        cx.op("pe", lambda g=g, ri=ri, pp=pp, lo=lo: nc.tensor.matmul(PS[pp][:, 0:128], lhsT=W1f[lo:lo + 64, ri, g, :], rhs=W3n[lo:lo + 64, ri, g, :], start=(ri == 0), stop=(ri == 1)),
                              reads=[R], writes=[r_PS[pp]])
                cx.op("dve", lambda pf=pf: nc.vector.tensor_tensor(out=tmpT[:], in0=PS[pf][:, 0:128], in1=mF[:], op=ALU.mult), reads=[r_PS[pf], R], writes=[r_tmpT])
                cx.op("pool", lambda pb_=pb_: nc.gpsimd.tensor_copy(out=tA[:, 0, :], in_=mB[:]), reads=[R], writes=[R]) if False else None
                cx.op("dve", lambda pb_=pb_: nc.vector.tensor_tensor(out=tA[:, 0, :], in0=PS[pb_][:, 0:128], in1=mB[:], op=ALU.mult), reads=[r_PS[pb_], R], writes=[R])
                cx.op("dve", lambda g=g: nc.vector.tensor_tensor(out=Tm[:, g, :], in0=tmpT[:], in1=tA[:, 0, :], op=ALU.add), reads=[r_tmpT, R], writes=[r_Tm])
        cx.barrier()

        with ExitStack() as es2:
            def sb(name, shape, dt=F32):
                return es2.enter_context(nc.sbuf_tensor(name, shape, dt))
            HG = 8
            RS = sb("RS", [128, 2, HG, NSEG + 1, SEGL])
            r_RSc = [[Res("RSf_re"), Res("RSf_im")], [Res("RSb_re"), Res("RSb_im")]]
            r_tt = [[Res("tt%d_%d" % (d, i)) for i in range(4)] for d in range(2)]
            Uf = sb("Uf", [128, HG, NJ]); r_Uf = [Res("Uf%d" % i) for i in range(HG)]
            Ub = sb("Ub", [128, HG, NJ], BF16); r_Ub = [Res("Ub%d" % i) for i in range(HG)]
            Sin = [sb("Sin%d" % i, [128, 2, NJ], BF16) for i in range(2)]; r_Sin = [Res("Sin%d" % i) for i in range(2)]
            Z = sb("Z", [128, 2, HG, NSEG]); r_Z = [Res("Zf"), Res("Zb")]
            tt = [[sb("tt%d_%d" % (d, i), [128, HG, NSEG + 1]) for i in range(4)] for d in range(2)]
            tz = [[sb("tz%d_%d" % (d, i), [128, HG]) for i in range(2)] for d in range(2)]
            tc_ = [[sb("tc%d_%d" % (d, i), [128, HG, SEGL]) for i in range(2)] for d in range(2)]
            yo = [sb("yo%d" % i, [128, 512]) for i in range(2)]; r_yo = [Res("yo%d" % i) for i in range(2)]
            yt = [sb("yt%d" % i, [128, 512]) for i in range(2)]; r_yt = [Res("yt%d" % i) for i in range(2)]
            for i in range(2):
                cx.op("pool", lambda i=i: nc.gpsimd.memset(Sin[i][:], 0.0), writes=[r_Sin[i]])
            ENG = ["dve", "pool"]
            VE = [nc.vector, nc.gpsimd]
            for half in range(NG // HG):
                g0 = half * HG
                for d in range(2):
                    rows = slice(64 * d, 64 * d + 64)
                    e = ENG[d]; V = VE[d]
                    cx.op(e, lambda V=V, rows=rows: V.memset(RS[rows, :, :, NSEG, :], 0.0), writes=r_RSc[d])
                    i0 = 0 if d == 0 else SEGL - 1
                    cx.op(e, lambda V=V, rows=rows, i0=i0, g0=g0: V.tensor_copy(out=RS[rows, :, :, NSEG, i0], in_=a8[rows, :, g0:g0 + HG]), reads=[r_a8], writes=r_RSc[d])
                for gl in range(HG):
                    g = g0 + gl
                    cx.dma("sp", lambda gl=gl, g=g: Uf[:, gl, :], lambda g=g: U[g, :, :], writes=[r_Uf[gl]], key="d_U%d" % (gl % 2))
                    cx.op("act", lambda gl=gl: nc.scalar.copy(out=Ub[:, gl, :], in_=Uf[:, gl, :]), reads=[r_Uf[gl]], writes=[r_Ub[gl]])
                    for ri in range(2):
                        for blk in range(2):
                            pi = (ri * 2 + blk) % 4
                            cx.op("pe", lambda gl=gl, g=g, ri=ri, blk=blk, pi=pi: nc.tensor.matmul(PS[pi][:], lhsT=W1T[:, g, ri, :], rhs=Ub[:, gl, blk * 512:(blk + 1) * 512], start=True, stop=True),
                                  reads=[r_W1T, r_Ub[gl]], writes=[r_PS[pi]])
                            cx.op("act", lambda gl=gl, ri=ri, blk=blk, pi=pi: nc.scalar.copy(out=RS[:, ri, gl, blk * SPB:(blk + 1) * SPB, :], in_=PS[pi][:].rearrange("p (s i) -> p s i", i=SEGL)),
                                  reads=[r_PS[pi]], writes=r_RSc[0] + r_RSc[1])
                for step in range(1, SEGL):
                    for d in range(2):
                        rows = slice(64 * d, 64 * d + 64)
                        e = ENG[d]; V = VE[d]
                        i = step if d == 0 else SEGL - 1 - step
                        ip = i - 1 if d == 0 else i + 1
                        ar_ = lambda rows=rows: a8[rows, 0, g0:g0 + HG].unsqueeze(2).to_broadcast([64, HG, NSEG + 1])
                        ai_ = lambda rows=rows: a8[rows, 1, g0:g0 + HG].unsqueeze(2).to_broadcast([64, HG, NSEG + 1])
                        T = tt[d]
                        rre, rim = r_RSc[d]
                        rT = r_tt[d]
                        cx.op(e, lambda V=V, rows=rows, ip=ip, T=T, ai_=ai_: V.tensor_tensor(out=T[1][rows], in0=RS[rows, 1, :, :, ip], in1=ai_(), op=ALU.mult), reads=[rim, r_a8], writes=[rT[1]])
                        cx.op(e, lambda V=V, rows=rows, ip=ip, T=T, ar_=ar_: V.tensor_tensor(out=T[0][rows], in0=RS[rows, 0, :, :, ip], in1=ar_(), op=ALU.mult), reads=[rre, r_a8], writes=[rT[0]])
                        cx.op(e, lambda V=V, rows=rows, ip=ip, T=T, ai_=ai_: V.tensor_tensor(out=T[2][rows], in0=RS[rows, 0, :, :, ip], in1=ai_(), op=ALU.mult), reads=[rre, r_a8], writes=[rT[2]])
                        cx.op(e, lambda V=V, rows=rows, ip=ip, T=T, ar_=ar_: V.tensor_tensor(out=T[3][rows], in0=RS[rows, 1, :, :, ip], in1=ar_(), op=ALU.mult), reads=[rim, r_a8], writes=[rT[3]])
                        cx.op(e, lambda V=V, rows=rows, T=T: V.tensor_tensor(out=T[2][rows], in0=T[2][rows], in1=T[3][rows], op=ALU.add), reads=[rT[2], rT[3]], writes=[rT[2]])
                        cx.op(e, lambda V=V, rows=rows, T=T: V.tensor_tensor(out=T[0][rows], in0=T[0][rows], in1=T[1][rows], op=ALU.subtract), reads=[rT[0], rT[1]], writes=[rT[0]])
                        cx.op(e, lambda V=V, rows=rows, i=i, T=T: V.tensor_tensor(out=RS[rows, 0, :, :, i], in0=RS[rows, 0, :, :, i], in1=T[0][rows], op=ALU.add), reads=[rT[0], rre], writes=[rre])
                        cx.op(e, lambda V=V, rows=rows, i=i, T=T: V.tensor_tensor(out=RS[rows, 1, :, :, i], in0=RS[rows, 1, :, :, i], in1=T[2][rows], op=ALU.add), reads=[rT[2], rim], writes=[rim])
                for d in range(2):
                    rows = slice(64 * d, 64 * d + 64)
                    e = ENG[d]; V = VE[d]
                    last = SEGL - 1 if d == 0 else 0
                    order = list(range(NSEG)) if d == 0 else list(range(NSEG - 1, -1, -1))
                    rr = r_RSc[d] + [r_Z[d]]
                    def o(fn, e=e, rr=rr):
                        cx.op(e, fn, reads=rr, writes=rr)
                    pr = lambda rows=rows, last=last: RS[rows, 0, :, NSEG, last]
                    pim = lambda rows=rows, last=last: RS[rows, 1, :, NSEG, last]
                    for n, m in enumerate(order):
                        o(lambda V=V, rows=rows, m=m, last=last: V.tensor_copy(out=Z[rows, :, :, m], in_=RS[rows, :, :, m, last]))
                        if n > 0:
                            mp = order[n - 1]
                            A, B = tz[d]
                            o(lambda V=V, rows=rows, mp=mp, A=A, pr=pr: V.tensor_tensor(out=A[rows], in0=Z[rows, 0, :, mp], in1=pr(), op=ALU.mult))
                            o(lambda V=V, rows=rows, mp=mp, B=B, pim=pim: V.tensor_tensor(out=B[rows], in0=Z[rows, 1, :, mp], in1=pim(), op=ALU.mult))
                            o(lambda V=V, rows=rows, A=A, B=B: V.tensor_tensor(out=A[rows], in0=A[rows], in1=B[rows], op=ALU.subtract))
                            o(lambda V=V, rows=rows, m=m, A=A: V.tensor_tensor(out=Z[rows, 0, :, m], in0=Z[rows, 0, :, m], in1=A[rows], op=ALU.add))
                            o(lambda V=V, rows=rows, mp=mp, A=A, pim=pim: V.tensor_tensor(out=A[rows], in0=Z[rows, 0, :, mp], in1=pim(), op=ALU.mult))
                            o(lambda V=V, rows=rows, mp=mp, B=B, pr=pr: V.tensor_tensor(out=B[rows], in0=Z[rows, 1, :, mp], in1=pr(), op=ALU.mult))
                            o(lambda V=V, rows=rows, A=A, B=B: V.tensor_tensor(out=A[rows], in0=A[rows], in1=B[rows], op=ALU.add))
                            o(lambda V=V, rows=rows, m=m, A=A: V.tensor_tensor(out=Z[rows, 1, :, m], in0=Z[rows, 1, :, m], in1=A[rows], op=ALU.add))
                    shp = [64, HG, SEGL]
                    A, B = tc_[d]
                    for m in (range(1, NSEG) if d == 0 else range(0, NSEG - 1)):
                        mz = m - 1 if d == 0 else m + 1
                        Pb = lambda ri, rows=rows: RS[rows, ri, :, NSEG, :]
                        Zb = lambda ri, rows=rows, mz=mz: Z[rows, ri, :, mz:mz + 1].to_broadcast(shp)
                        o(lambda V=V, rows=rows, A=A, Pb=Pb, Zb=Zb: V.tensor_tensor(out=A[rows], in0=Pb(0), in1=Zb(0), op=ALU.mult))
                        o(lambda V=V, rows=rows, B=B, Pb=Pb, Zb=Zb: V.tensor_tensor(out=B[rows], in0=Pb(1), in1=Zb(1), op=ALU.mult))
                        o(lambda V=V, rows=rows, A=A, B=B: V.tensor_tensor(out=A[rows], in0=A[rows], in1=B[rows], op=ALU.subtract))
                        o(lambda V=V, rows=rows, A=A, m=m: V.tensor_tensor(out=RS[rows, 0, :, m, :], in0=RS[rows, 0, :, m, :], in1=A[rows], op=ALU.add))
                        o(lambda V=V, rows=rows, A=A, Pb=Pb, Zb=Zb: V.tensor_tensor(out=A[rows], in0=Pb(0), in1=Zb(1), op=ALU.mult))
                        o(lambda V=V, rows=rows, B=B, Pb=Pb, Zb=Zb: V.tensor_tensor(out=B[rows], in0=Pb(1), in1=Zb(0), op=ALU.mult))
                        o(lambda V=V, rows=rows, A=A, B=B: V.tensor_tensor(out=A[rows], in0=A[rows], in1=B[rows], op=ALU.add))
                        o(lambda V=V, rows=rows, A=A, m=m: V.tensor_tensor(out=RS[rows, 1, :, m, :], in0=RS[rows, 1, :, m, :], in1=A[rows], op=ALU.add))
                for gl in range(HG):
                    g = g0 + gl
                    sbi = gl % 2
                    S_ = Sin[sbi]
                    for ri in range(2):
                        src = lambda rows, ri=ri, gl=gl: RS[rows, ri, gl, 0:NSEG, :].rearrange("p s i -> p (s i)")
                        cx.op("act", lambda S_=S_, ri=ri, src=src: nc.scalar.copy(out=S_[0:64, ri, 1:NJ], in_=src(slice(0, 64))[:, 0:NJ - 1]), reads=r_RSc[0], writes=[r_Sin[sbi]])
                        cx.op("act", lambda S_=S_, ri=ri, src=src: nc.scalar.copy(out=S_[64:128, ri, 0:NJ - 1], in_=src(slice(64, 128))[:, 1:NJ]), reads=r_RSc[1], writes=[r_Sin[sbi]])
                    for blk in range(2):
                        pi = 4 + (gl * 2 + blk) % 4
                        yb = (gl * 2 + blk) % 2
                        cs = slice(blk * 512, (blk + 1) * 512)
                        cx.op("pe", lambda g=g, gl=gl, cs=cs, pi=pi: nc.tensor.matmul(PS[pi][:], lhsT=Tm[:, g, :], rhs=Ub[:, gl, cs], start=True, stop=False), reads=[r_Tm, r_Ub[gl]], writes=[r_PS[pi]])
                        cx.op("pe", lambda g=g, S_=S_, cs=cs, pi=pi: nc.tensor.matmul(PS[pi][:], lhsT=W3[:, g, 0, :], rhs=S_[:, 0, cs], start=False, stop=False), reads=[r_W3, r_Sin[sbi]], writes=[r_PS[pi]])
                        cx.op("pe", lambda g=g, S_=S_, cs=cs, pi=pi: nc.tensor.matmul(PS[pi][:], lhsT=W3[:, g, 1, :], rhs=S_[:, 1, cs], start=False, stop=True), reads=[r_W3, r_Sin[sbi]], writes=[r_PS[pi]])
                        cx.op("dve", lambda g=g, gl=gl, cs=cs, pi=pi, yb=yb: nc.vector.scalar_tensor_tensor(out=yt[yb][:], in0=Uf[:, gl, cs], scalar=dskt[:, g:g + 1], in1=PS[pi][:], op0=ALU.mult, op1=ALU.add),
                              reads=[r_Uf[gl], r_dsk, r_PS[pi]], writes=[r_yt[yb]])
                        cx.op("act", lambda yb=yb: nc.scalar.activation(out=yo[yb][:], in_=yt[yb][:], func=AF.Gelu_apprx_tanh), reads=[r_yt[yb]], writes=[r_yo[yb]])
                        cx.dma("sp", lambda g=g, cs=cs: Gy[g, :, cs], lambda yb=yb: yo[yb][:], reads=[r_yo[yb]], key="d_yo%d" % yb)
    cx.finish("sp")


def host_inputs_D(inp, u_full, core):
    g0 = core * NG
    uu = u_full[:, g0 * 16:(g0 + NG) * 16].reshape(NJ, 8, NG, 16)
    U = np.ascontiguousarray(uu.transpose(2, 1, 3, 0).reshape(NG, 128, NJ))
    def qg(a):
        return np.ascontiguousarray(a[:, g0:g0 + NG, :].transpose(0, 2, 1).reshape(128, NG))
    lamre = qg(inp["lam_re"][0]); lamim = qg(inp["lam_im"][0])
    logdt = np.ascontiguousarray(np.broadcast_to(inp["log_dt"][0][:, None, g0:g0 + NG], (2, 64, NG)).reshape(128, NG))
    def bq(a):
        return np.ascontiguousarray(a[:, g0:g0 + NG].transpose(0, 2, 1, 3).reshape(128, NG, 16))
    def cq(a):
        return np.ascontiguousarray(a[:, g0:g0 + NG].transpose(0, 3, 1, 2).reshape(128, NG, 16))
    dsk = np.ascontiguousarray(np.broadcast_to(inp["d_skip"][0][g0 * 16:(g0 + NG) * 16].reshape(NG, 16).T[None], (8, 16, NG)).reshape(128, NG))
    s_idx = np.arange(128) // 16
    maskF = (s_idx[None, :] >= s_idx[:, None]).astype(np.float32)
    maskB = (s_idx[None, :] <= s_idx[:, None]).astype(np.float32)
    return {"U": U, "lamre": lamre, "lamim": lamim, "logdt": logdt, "Bre": bq(inp["b_re"][0]), "Bim": bq(inp["b_im"][0]),
            "Cre": cq(inp["c_re"][0]), "Cim": cq(inp["c_im"][0]), "dsk": dsk, "maskF": maskF, "maskB": maskB, "ident": np.eye(128, dtype=np.float32)}


def build_E(nc, cx):
    def dram(name, shape, dt, kind):
        return nc.dram_tensor(name, shape, dt, kind=kind).ap()
    gT = dram("gT", [2048, S_LOC], F32, "ExternalInput")
    gtok = dram("gtok", [S_LOC, 2048], F32, "ExternalInput")
    gz = dram("gz", [S_LOC, 2048], F32, "ExternalInput")
    x = dram("x", [S_LOC, 2048], F32, "ExternalInput")
    w_glu = dram("w_glu", [2048, 2048], F32, "ExternalInput")
    b_glu = dram("b_glu", [1, 2048], F32, "ExternalInput")
    w_out = dram("w_out", [2048, 2048], F32, "ExternalInput")
    ln_g = dram("ln_g", [1, 2048], F32, "ExternalInput")
    ln_b = dram("ln_b", [1, 2048], F32, "ExternalInput")
    ident = dram("ident", [128, 128], F32, "ExternalInput")
    out = dram("out", [S_LOC, 2048], F32, "ExternalOutput")
    with ExitStack() as es0:
        def sb0(name, shape, dt):
            return es0.enter_context(nc.sbuf_tensor(name, shape, dt))
        def ps0(name, shape, dt=F32):
            return es0.enter_context(nc.psum_tensor(name, shape, dt))
        og = sb0("og", [128, 8, 2048], BF16); r_og = [Res("og%d" % i) for i in range(8)]
        identb = sb0("identb", [128, 128], BF16); r_id = Res("identb")
        cx.dma("pool", lambda: identb[:], lambda: ident[:, :], writes=[r_id], key="d_c")
        psT = [ps0("sps%d" % i, [128, 1024]) for i in range(2)]; r_psT = [Res("sps%d" % i) for i in range(2)]
        O = [ps0("ops%d" % i, [128, 512]) for i in range(4)]; r_O = [Res("ops%d" % i) for i in range(4)]
        with ExitStack() as es1:
            def sb(name, shape, dt):
                return es1.enter_context(nc.sbuf_tensor(name, shape, dt))
            gTb = sb("gTb", [128, 16, S_LOC], BF16); r_gT = [Res("gTb%d" % c) for c in range(16)]
            wg = sb("wg", [128, 16, 2048], BF16); r_wg = [Res("wg%d" % i) for i in range(4)]
            bgt = sb("bgt", [128, 2048], F32); r_bg = Res("bgt")
            gk = [sb("gk%d" % i, [128, 2048], F32) for i in range(2)]; r_gk = [Res("gk%d" % i) for i in range(2)]
            zk = [sb("zk%d" % i, [128, 2048], F32) for i in range(2)]; r_zk = [Res("zk%d" % i) for i in range(2)]
            tg = [sb("tg%d" % i, [128, 512], F32) for i in range(3)]; r_tg = [Res("tg%d" % i) for i in range(3)]
            for c in range(16):
                cx.dma("pool", lambda c=c: gTb[:, c, :], lambda c=c: gT[c * 128:(c + 1) * 128, :], writes=[r_gT[c]], key="d_g")
            wv = w_glu.rearrange("(c p) n -> p c n", p=128)
            for cb in range(4):
                cx.dma("pool", lambda cb=cb: wg[:, :, cb * 512:(cb + 1) * 512], lambda cb=cb: wv[:, :, cb * 512:(cb + 1) * 512], writes=[r_wg[cb]], key="d_wg")
            cx.dma("sp", lambda: bgt[:], lambda: b_glu[0:1, :].to_broadcast([128, 2048]), writes=[r_bg], key="d_c2")
            it = 0
            for st in range(8):
                b = st % 2
                cx.dma("sp", lambda st=st, b=b: gk[b][:], lambda st=st: gtok[st * 128:(st + 1) * 128, :], writes=[r_gk[b]], key="d_gk%d" % b)
                cx.dma("act", lambda st=st, b=b: zk[b][:], lambda st=st: gz[st * 128:(st + 1) * 128, :], writes=[r_zk[b]], key="d_zk%d" % b)
                for cb in range(4):
                    P = O[cb]; ti = it % 3; it += 1
                    cs = slice(cb * 512, (cb + 1) * 512)
                    for c in range(16):
                        cx.op("pe", lambda c=c, P=P, st=st, cs=cs: nc.tensor.matmul(P[:], lhsT=gTb[:, c, st * 128:(st + 1) * 128], rhs=wg[:, c, cs], start=(c == 0), stop=(c == 15)),
                              reads=[r_gT[c], r_wg[cb]], writes=[r_O[cb]])
                    cx.op("dve", lambda P=P, ti=ti, cs=cs: nc.vector.tensor_tensor(out=tg[ti][:], in0=P[:], in1=bgt[:, cs], op=ALU.add), reads=[r_O[cb], r_bg], writes=[r_tg[ti]])
                    cx.op("act", lambda ti=ti: nc.scalar.activation(out=tg[ti][:], in_=tg[ti][:], func=AF.Sigmoid), reads=[r_tg[ti]], writes=[r_tg[ti]])
                    cx.op("dve", lambda ti=ti, b=b, cs=cs: nc.vector.tensor_tensor(out=tg[ti][:], in0=tg[ti][:], in1=gk[b][:, cs], op=ALU.mult), reads=[r_tg[ti], r_gk[b]], writes=[r_tg[ti]])
                    cx.op("pool", lambda ti=ti, b=b, cs=cs, st=st: nc.gpsimd.tensor_tensor(out=og[:, st, cs], in0=tg[ti][:], in1=zk[b][:, cs], op=ALU.mult), reads=[r_tg[ti], r_zk[b]], writes=[r_og[st]])
        cx.barrier()
        with ExitStack() as es2:
            ogT = es2.enter_context(nc.sbuf_tensor("ogT", [128, 16, S_LOC], BF16)); r_ogT = [Res("ogT%d" % i) for i in range(16)]
            transposes_tok2feat(nc, cx, og, r_og, ogT, r_ogT, identb, r_id, psT, r_psT)
            outproj_ln(nc, cx, es2, ogT, r_ogT, w_out, x, ln_g, ln_b, out, O, r_O)
    cx.finish("sp")


_PROGS = {}

def _prog(name, fn):
    if name not in _PROGS:
        _PROGS[name] = build2(fn)
    return _PROGS[name]


def kernel(x, ln_g, ln_b, w_in_attn, q_norm_g, k_norm_g, w_out_attn, w_in_ssm, lam_re, lam_im, log_dt,
           b_re, b_im, c_re, c_im, d_skip, w_glu, b_glu, w_out_ssm):
    NC = 8
    cores = list(range(NC))
    f32 = np.float32
    x0 = np.asarray(x, dtype=f32)[0]
    ident = np.eye(128, dtype=f32)
    inp = {"lam_re": np.asarray(lam_re, f32), "lam_im": np.asarray(lam_im, f32), "log_dt": np.asarray(log_dt, f32),
           "b_re": np.asarray(b_re, f32), "b_im": np.asarray(b_im, f32), "c_re": np.asarray(c_re, f32), "c_im": np.asarray(c_im, f32),
           "d_skip": np.asarray(d_skip, f32)}
    xs = [np.ascontiguousarray(x0[c * 1024:(c + 1) * 1024]) for c in cores]
    ims = []
    for c in cores:
        C, Sn, Sp = rope_tables(np.arange(c * 1024, (c + 1) * 1024))
        ims.append({"xT": np.ascontiguousarray(xs[c].T), "w_in": np.asarray(w_in_attn, f32)[0], "gq": np.asarray(q_norm_g, f32)[0][None, :],
                    "gk": np.asarray(k_norm_g, f32)[0][None, :], "ctab": C, "sntab": Sn, "sptab": Sp})
    rA = run_bass_kernel_spmd(_prog("A", lambda nc, cx: build_proj(nc, cx, "attn")), ims, core_ids=cores).results
    k_all = np.concatenate([np.asarray(r["k_o"]) for r in rA], axis=0)
    v_all = np.concatenate([np.asarray(r["v_o"]) for r in rA], axis=0)
    KT = np.ascontiguousarray(k_all.reshape(8192, 4, 128).transpose(1, 2, 0))
    V = np.ascontiguousarray(v_all.reshape(8192, 4, 128).transpose(1, 0, 2))
    ims = []
    for c in cores:
        q = np.asarray(rA[c]["q_o"])
        ims.append({"qT": np.ascontiguousarray(q.reshape(1024, 16, 128).transpose(1, 2, 0)), "KT": KT, "V": V,
                    "gz": np.asarray(rA[c]["gz_o"]), "x": xs[c], "w_out": np.asarray(w_out_attn, f32)[0],
                    "ln_g": np.asarray(ln_g, f32)[0:1], "ln_b": np.asarray(ln_b, f32)[0:1], "ident": ident})
    rB = run_bass_kernel_spmd(_prog("B", build_B), ims, core_ids=cores).results
    h1 = [np.asarray(r["h1"]) for r in rB]
    dummy = np.zeros((1, 128), f32)
    ims = []
    for c in cores:
        C, Sn, Sp = rope_tables(np.arange(c * 1024, (c + 1) * 1024))
        ims.append({"xT": np.ascontiguousarray(h1[c].T), "w_in": np.asarray(w_in_ssm, f32)[0], "gq": dummy, "gk": dummy, "ctab": C, "sntab": Sn, "sptab": Sp})
    rC = run_bass_kernel_spmd(_prog("C", lambda nc, cx: build_proj(nc, cx, "ssm")), ims, core_ids=cores).results
    u_full = np.concatenate([np.asarray(r["u_o"]) for r in rC], axis=0)
    ims = [host_inputs_D(inp, u_full, c) for c in cores]
    rD = run_bass_kernel_spmd(_prog("D", build_D), ims, core_ids=cores).results
    g_full = np.concatenate([np.asarray(r["Gy"]).reshape(NG, 8, 16, NJ).transpose(3, 1, 0, 2).reshape(8192, NG * 16) for r in rD], axis=1)
    ims = []
    for c in cores:
        gt = np.ascontiguousarray(g_full[c * 1024:(c + 1) * 1024])
        ims.append({"gT": np.ascontiguousarray(gt.T), "gtok": gt, "gz": np.asarray(rC[c]["gz_o"]), "x": h1[c],
                    "w_glu": np.asarray(w_glu, f32)[0], "b_glu": np.asarray(b_glu, f32)[0:1], "w_out": np.asarray(w_out_ssm, f32)[0],
                    "ln_g": np.asarray(ln_g, f32)[1:2], "ln_b": np.asarray(ln_b, f32)[1:2], "ident": ident})
    rE = run_bass_kernel_spmd(_prog("E", build_E), ims, core_ids=cores).results
    out = np.concatenate([np.asarray(r["out"]) for r in rE], axis=0)
    return out[None].astype(np.float32)
```

```python
import math
import ml_dtypes
import numpy as np
from contextlib import ExitStack
import concourse.bass as bass
import concourse.mybir as mybir
from concourse.bass_utils import run_bass_kernel_spmd

F32 = mybir.dt.float32
BF16 = mybir.dt.bfloat16
AF = mybir.ActivationFunctionType
ALU = mybir.AluOpType
AX = mybir.AxisListType
ENGS = ("pe", "act", "dve", "pool", "sp")


class Res:
    def __init__(self, name):
        self.name = name
        self.w = None
        self.r = {}


class Ctx:
    def __init__(self, nc, es, plan=None):
        self.nc = nc
        self.es = es
        self.plan = plan
        self.emit = plan is not None
        self.eng = {"pe": nc.tensor, "act": nc.scalar, "dve": nc.vector, "pool": nc.gpsimd, "sp": nc.sync}
        self.idx = {e: 0 for e in ENGS}
        self.seen = {e: {} for e in ENGS}
        self.record = set()
        self.dcount = {}
        self.dsem = {}
        self.esem = {}
        self.rank = {}
        if self.emit:
            for e in ENGS:
                self.esem[e] = es.enter_context(nc.semaphore("es_" + e))
                ids = sorted(i for (k, i) in plan if k == e)
                self.rank[e] = {i: n + 1 for n, i in enumerate(ids)}

    def _wait(self, e, key, val):
        if key == e and e == "pe":
            return
        if self.seen[e].get(key, -1) >= val:
            return
        self.seen[e][key] = val
        if key in ENGS:
            self.record.add((key, val))
            if self.emit:
                self.eng[e].wait_ge(self.esem[key], self.rank[key][val])
        else:
            val = max(val, self.dcount[key])
            self.seen[e][key] = val
            if self.emit:
                self.eng[e].wait_ge(self.dsem[key], val)

    def _deps(self, e, reads, writes):
        for r in reads:
            if r.w is not None:
                self._wait(e, *r.w)
        for w in writes:
            if w.w is not None:
                self._wait(e, *w.w)
            for k, v in w.r.items():
                self._wait(e, k, v)

    def _mark(self, ev, reads, writes):
        for r in reads:
            r.r[ev[0]] = ev[1]
        for w in writes:
            w.w = ev
            w.r = {}

    def op(self, e, fn, reads=(), writes=()):
        self._deps(e, reads, writes)
        i = self.idx[e]
        self.idx[e] = i + 1
        if self.emit:
            ins = fn()
            if (e, i) in self.plan:
                ins.then_inc(self.esem[e], 1)
        self.seen[e][e] = max(self.seen[e].get(e, -1), -1)
        self._mark((e, i), reads, writes)

    def dma(self, q, out, in_, reads=(), writes=(), key=None, **kw):
        self._deps(q, reads, writes)
        if key is None:
            key = "d_" + (writes[0].name if writes else reads[0].name)
        if key not in self.dcount:
            self.dcount[key] = 0
            if self.emit:
                self.dsem[key] = self.es.enter_context(self.nc.semaphore(key))
        elif self.dcount[key] > 0:
            self._wait(q, key, self.dcount[key])
        self.dcount[key] += 16
        if self.emit:
            self.eng[q].dma_start(out=out(), in_=in_(), **kw).then_inc(self.dsem[key], 16)
        self._mark((key, self.dcount[key]), reads, writes)

    def barrier(self):
        last = {e: self.idx[e] - 1 for e in ENGS if self.idx[e] > 0}
        for e in ENGS:
            for k, v in last.items():
                if k != e:
                    self._wait(e, k, v)
            for k, v in self.dcount.items():
                self._wait(e, k, v)

    def finish(self, e="sp"):
        for k, v in self.dcount.items():
            self._wait(e, k, v)


def build2(build_fn):
    nc1 = bass.Bass("TRN2", target_bir_lowering=False)
    with ExitStack() as es:
        c1 = Ctx(nc1, es, plan=None)
        build_fn(nc1, c1)
        plan = set(c1.record)
    nc2 = bass.Bass("TRN2", target_bir_lowering=False)
    with ExitStack() as es:
        c2 = Ctx(nc2, es, plan=plan)
        build_fn(nc2, c2)
    return nc2

S_LOC = 1024; DM = 2048; NT = 8
ATT_IN = 5120
QK_EPS = 1e-6

def build_proj(nc, cx, mode):
    es = cx.es
    ssm = (mode == 'ssm')
    WCOLS = 4096 if ssm else ATT_IN
    NCB = WCOLS // 512
    def dram(name, shape, dt, kind):
        return nc.dram_tensor(name, shape, dt, kind=kind).ap()
    xT = dram("xT", [DM, S_LOC], F32, "ExternalInput")
    w_in = dram("w_in", [DM, WCOLS], F32, "ExternalInput")
    gq = dram("gq", [1, 128], F32, "ExternalInput")
    gk = dram("gk", [1, 128], F32, "ExternalInput")
    ctab = dram("ctab", [S_LOC, 128], F32, "ExternalInput")
    sntab = dram("sntab", [S_LOC, 64], F32, "ExternalInput")
    sptab = dram("sptab", [S_LOC, 64], F32, "ExternalInput")
    if ssm:
        u_o = dram("u_o", [S_LOC, 2048], F32, "ExternalOutput")
    else:
        q_o = dram("q_o", [S_LOC, 2048], BF16, "ExternalOutput")
        k_o = dram("k_o", [S_LOC, 512], BF16, "ExternalOutput")
        v_o = dram("v_o", [S_LOC, 512], BF16, "ExternalOutput")
    gz_o = dram("gz_o", [S_LOC, 2048], F32, "ExternalOutput")

    def sb(name, shape, dt):
        return es.enter_context(nc.sbuf_tensor(name, shape, dt))
    def ps(name, shape, dt=F32):
        return es.enter_context(nc.psum_tensor(name, shape, dt))

    xTb = sb("xTb", [128, 16, S_LOC], BF16); r_x = [Res("xTb%d" % c) for c in range(16)]
    wb = [sb("wb%d" % i, [128, 16, 512], BF16) for i in range(2)]; r_w = [Res("wb%d" % i) for i in range(2)]
    gqt = sb("gqt", [128, 128], F32); r_gq = Res("gqt")
    gkt = sb("gkt", [128, 128], F32); r_gk = Res("gkt")
    ct = sb("ct", [128, NT, 128], F32); snt = sb("snt", [128, NT, 64], F32); spt = sb("spt", [128, NT, 64], F32)
    r_tab = Res("tab")
    pss = [ps("ps%d" % i, [128, 512]) for i in range(4)]; r_ps = [Res("ps%d" % i) for i in range(4)]
    NB = 3
    ss = [sb("ss%d" % i, [128, 4], F32) for i in range(NB)]; r_ss = [Res("ss%d" % i) for i in range(NB)]
    junk = sb("junk", [128, 128], F32); r_junk = Res("junk")
    epst = sb("epst", [128, 1], F32)
    r_eps = Res("epst")
    cx.op("dve", lambda: nc.vector.memset(epst[:], QK_EPS), writes=[r_eps])
    qn = [sb("qn%d" % i, [128, 128], F32) for i in range(NB)]; r_qn = [Res("qn%d" % i) for i in range(NB)]
    t1 = [sb("t1_%d" % i, [128, 128], F32) for i in range(NB)]; r_t1 = [Res("t1_%d" % i) for i in range(NB)]
    t2 = [sb("t2_%d" % i, [128, 128], F32) for i in range(NB)]; r_t2 = [Res("t2_%d" % i) for i in range(NB)]
    ob = [sb("ob%d" % i, [128, 512], BF16) for i in range(NB)]; r_ob = [Res("ob%d" % i) for i in range(NB)]
    of = [sb("of%d" % i, [128, 512], F32) for i in range(NB)]; r_of = [Res("of%d" % i) for i in range(NB)]

    for c in range(16):
        cx.dma("pool", lambda c=c: xTb[:, c, :], lambda c=c: xT[c * 128:(c + 1) * 128, :], writes=[r_x[c]], key="d_x%d" % c)
    cx.dma("sp", lambda: gqt[:], lambda: gq[0:1, :].to_broadcast([128, 128]), writes=[r_gq], key="d_c")
    cx.dma("sp", lambda: gkt[:], lambda: gk[0:1, :].to_broadcast([128, 128]), writes=[r_gk], key="d_c")
    cx.dma("sp", lambda: ct[:], lambda: ctab.rearrange("(t p) n -> p t n", p=128), writes=[r_tab], key="d_c")
    cx.dma("sp", lambda: snt[:], lambda: sntab.rearrange("(t p) n -> p t n", p=128), writes=[r_tab], key="d_c")
    cx.dma("sp", lambda: spt[:], lambda: sptab.rearrange("(t p) n -> p t n", p=128), writes=[r_tab], key="d_c")

    w_view = w_in.rearrange("(c p) n -> p c n", p=128)
    def load_w(cb):
        b = cb % 2
        cx.dma("pool", lambda: wb[b][:], lambda: w_view[:, :, cb * 512:(cb + 1) * 512], writes=[r_w[b]], key="d_w%d" % b)
    load_w(0)
    it = 0
    for cb in range(NCB):
        if cb + 1 < NCB:
            load_w(cb + 1)
        b = cb % 2
        for t in range(NT):
            pi = it % 4; bi = it % NB; it += 1
            P = pss[pi]
            for c in range(16):
                cx.op("pe", lambda c=c, P=P, t=t, b=b: nc.tensor.matmul(P[:], lhsT=xTb[:, c, t * 128:(t + 1) * 128], rhs=wb[b][:, c, :], start=(c == 0), stop=(c == 15)),
                      reads=[r_x[c], r_w[b]], writes=[r_ps[pi]])
            if ssm and cb < 4:
                cx.op("act", lambda P=P, bi=bi: nc.scalar.copy(out=of[bi][:], in_=P[:]), reads=[r_ps[pi]], writes=[r_of[bi]])
                cx.dma("sp", lambda t=t, cb=cb: u_o[t * 128:(t + 1) * 128, cb * 512:(cb + 1) * 512], lambda bi=bi: of[bi][:], reads=[r_of[bi]], key="d_of%d" % bi)
            elif ssm:
                cx.op("act", lambda P=P, bi=bi: nc.scalar.activation(out=of[bi][:], in_=P[:], func=AF.Silu), reads=[r_ps[pi]], writes=[r_of[bi]])
                cx.dma("sp", lambda t=t, cb=cb: gz_o[t * 128:(t + 1) * 128, (cb - 4) * 512:(cb - 3) * 512], lambda bi=bi: of[bi][:], reads=[r_of[bi]], key="d_of%d" % bi)
            elif cb <= 4:
                g_t, r_g = (gqt, r_gq) if cb < 4 else (gkt, r_gk)
                for h in range(4):
                    cx.op("act", lambda h=h, P=P, bi=bi: nc.scalar.activation(out=junk[:], in_=P[:, h * 128:(h + 1) * 128], func=AF.Square, accum_out=ss[bi][:, h:h + 1]),
                          reads=[r_ps[pi]], writes=[r_junk, r_ss[bi]])
                cx.op("act", lambda bi=bi: nc.scalar.activation(out=ss[bi][:], in_=ss[bi][:], func=AF.Sqrt, scale=1.0 / 128, bias=epst[:, 0:1]),
                      reads=[r_ss[bi], r_eps], writes=[r_ss[bi]])
                cx.op("dve", lambda bi=bi: nc.vector.reciprocal(out=ss[bi][:], in_=ss[bi][:]),
                      reads=[r_ss[bi]], writes=[r_ss[bi]])
                for h in range(4):
                    hb = (it * 4 + h) % NB
                    cx.op("dve", lambda h=h, P=P, bi=bi, hb=hb, g_t=g_t: nc.vector.scalar_tensor_tensor(out=qn[hb][:], in0=P[:, h * 128:(h + 1) * 128], scalar=ss[bi][:, h:h + 1], in1=g_t[:], op0=ALU.mult, op1=ALU.mult),
                          reads=[r_ps[pi], r_ss[bi], r_g], writes=[r_qn[hb]])
                    cx.op("pool", lambda hb=hb, t=t: nc.gpsimd.tensor_tensor(out=t1[hb][:], in0=qn[hb][:], in1=ct[:, t, :], op=ALU.mult),
                          reads=[r_qn[hb], r_tab], writes=[r_t1[hb]])
                    def v4(ap):
                        return ap.rearrange("p (a h m) -> p a h m", a=2, h=2)
                    def v3(ap):
                        return ap.rearrange("p (a m) -> p a m", a=2)
                    cx.op("dve", lambda hb=hb, t=t: nc.vector.tensor_tensor(out=v4(t2[hb][:])[:, :, 0, :], in0=v4(qn[hb][:])[:, :, 1, :], in1=v3(snt[:, t, :]), op=ALU.mult),
                          reads=[r_qn[hb], r_tab], writes=[r_t2[hb]])
                    cx.op("dve", lambda hb=hb, t=t: nc.vector.tensor_tensor(out=v4(t2[hb][:])[:, :, 1, :], in0=v4(qn[hb][:])[:, :, 0, :], in1=v3(spt[:, t, :]), op=ALU.mult),
                          reads=[r_qn[hb], r_tab], writes=[r_t2[hb]])
                    cx.op("dve", lambda hb=hb, h=h, bi=bi: nc.vector.tensor_tensor(out=ob[bi][:, h * 128:(h + 1) * 128], in0=t1[hb][:], in1=t2[hb][:], op=ALU.add),
                          reads=[r_t1[hb], r_t2[hb]], writes=[r_ob[bi]])
                dst = q_o[t * 128:(t + 1) * 128, cb * 512:(cb + 1) * 512] if cb < 4 else k_o[t * 128:(t + 1) * 128, :]
                cx.dma("sp", lambda dst=dst: dst, lambda bi=bi: ob[bi][:], reads=[r_ob[bi]], key="d_ob%d" % bi)
            elif cb == 5:
                cx.op("act", lambda P=P, bi=bi: nc.scalar.copy(out=ob[bi][:], in_=P[:]), reads=[r_ps[pi]], writes=[r_ob[bi]])
                cx.dma("sp", lambda t=t: v_o[t * 128:(t + 1) * 128, :], lambda bi=bi: ob[bi][:], reads=[r_ob[bi]], key="d_ob%d" % bi)
            else:
                cx.op("act", lambda P=P, bi=bi: nc.scalar.activation(out=of[bi][:], in_=P[:], func=AF.Silu), reads=[r_ps[pi]], writes=[r_of[bi]])
                cx.dma("sp", lambda t=t, cb=cb: gz_o[t * 128:(t + 1) * 128, (cb - 6) * 512:(cb - 5) * 512], lambda bi=bi: of[bi][:], reads=[r_of[bi]], key="d_of%d" % bi)
    cx.finish("sp")


def rope_tables(pos):
    row = (pos // 64).astype(np.float32); col = (pos % 64).astype(np.float32)
    inv = (10000.0 ** (-np.arange(0, 64, 2, dtype=np.float32) / 64)).astype(np.float32)
    ar = row[:, None] * inv[None, :]; ac = col[:, None] * inv[None, :]
    cr, sr, cc, sc = np.cos(ar), np.sin(ar), np.cos(ac), np.sin(ac)
    C = np.concatenate([cr, cr, cc, cc], axis=1).astype(np.float32)
    Sn = np.concatenate([-sr, -sc], axis=1).astype(np.float32)
    Sp = np.concatenate([sr, sc], axis=1).astype(np.float32)
    return C, Sn, Sp


LN_EPS = 1e-5
ALPHA = (2.0 * 2) ** 0.25
SCALE = 128 ** -0.5

def outproj_ln(nc, cx, es, srcT, r_srcT, w_dram, x_dram, g_dram, b_dram, out_dram, ps_list, r_ps_list):
    def sb(name, shape, dt):
        return es.enter_context(nc.sbuf_tensor(name, shape, dt))
    wo = sb("wo", [128, 16, 2048], BF16); r_wo = [Res("wo%d" % i) for i in range(4)]
    gt = sb("lng", [128, 2048], F32); bt = sb("lnb", [128, 2048], F32); r_gb = Res("lngb")
    xt = [sb("xres%d" % i, [128, 2048], F32) for i in range(2)]; r_xt = [Res("xres%d" % i) for i in range(2)]
    rt = [sb("rres%d" % i, [128, 2048], F32) for i in range(2)]; r_rt = [Res("rres%d" % i) for i in range(2)]
    st6 = [sb("st6_%d" % i, [128, 4, 6], F32) for i in range(2)]; r_st6 = [Res("st6_%d" % i) for i in range(2)]
    mv = [sb("mv%d" % i, [128, 4], F32) for i in range(2)]; r_mv = [Res("mv%d" % i) for i in range(2)]
    epsl = sb("epsl", [128, 1], F32); r_epsl = Res("epsl")
    cx.op("dve", lambda: nc.vector.memset(epsl[:], LN_EPS), writes=[r_epsl])
    wv = w_dram.rearrange("(c p) n -> p c n", p=128)
    for cb in range(4):
        cx.dma("pool", lambda cb=cb: wo[:, :, cb * 512:(cb + 1) * 512], lambda cb=cb: wv[:, :, cb * 512:(cb + 1) * 512], writes=[r_wo[cb]], key="d_wo%d" % cb)
    cx.dma("sp", lambda: gt[:], lambda: g_dram[0:1, :].to_broadcast([128, 2048]), writes=[r_gb], key="d_c2")
    cx.dma("sp", lambda: bt[:], lambda: b_dram[0:1, :].to_broadcast([128, 2048]), writes=[r_gb], key="d_c2")
    for st in range(8):
        b = st % 2
        cx.dma("sp", lambda st=st, b=b: xt[b][:], lambda st=st: x_dram[st * 128:(st + 1) * 128, :], writes=[r_xt[b]], key="d_xr%d" % b)
        for cb in range(4):
            P = ps_list[cb]
            for c in range(16):
                cx.op("pe", lambda c=c, P=P, st=st, cb=cb: nc.tensor.matmul(P[:], lhsT=srcT[:, c, st * 128:(st + 1) * 128], rhs=wo[:, c, cb * 512:(cb + 1) * 512], start=(c == 0), stop=(c == 15)),
                      reads=[r_srcT[c], r_wo[cb]], writes=[r_ps_list[cb]])
            cx.op("dve", lambda P=P, b=b, cb=cb: nc.vector.scalar_tensor_tensor(out=rt[b][:, cb * 512:(cb + 1) * 512], in0=xt[b][:, cb * 512:(cb + 1) * 512], scalar=ALPHA, in1=P[:], op0=ALU.mult, op1=ALU.add),
                  reads=[r_xt[b], r_ps_list[cb]], writes=[r_rt[b]])
            cx.op("dve", lambda b=b, cb=cb: nc.vector.bn_stats(out=st6[b][:, cb, :], in_=rt[b][:, cb * 512:(cb + 1) * 512]), reads=[r_rt[b]], writes=[r_st6[b]])
        cx.op("dve", lambda b=b: nc.vector.bn_aggr(out=mv[b][:, 0:2], in_=st6[b][:].rearrange("p a s -> p (a s)")), reads=[r_st6[b]], writes=[r_mv[b]])
        cx.op("act", lambda b=b: nc.scalar.activation(out=mv[b][:, 2:3], in_=mv[b][:, 1:2], func=AF.Sqrt, bias=epsl[:, 0:1], scale=1.0), reads=[r_mv[b], r_epsl], writes=[r_mv[b]])
        cx.op("dve", lambda b=b: nc.vector.reciprocal(out=mv[b][:, 2:3], in_=mv[b][:, 2:3]), reads=[r_mv[b]], writes=[r_mv[b]])
        cx.op("dve", lambda b=b: nc.vector.tensor_scalar(out=mv[b][:, 3:4], in0=mv[b][:, 0:1], scalar1=mv[b][:, 2:3], scalar2=-1.0, op0=ALU.mult, op1=ALU.mult), reads=[r_mv[b]], writes=[r_mv[b]])
        cx.op("act", lambda b=b: nc.scalar.activation(out=xt[b][:], in_=rt[b][:], func=AF.Identity, scale=mv[b][:, 2:3], bias=mv[b][:, 3:4]), reads=[r_rt[b], r_mv[b]], writes=[r_xt[b]])
        cx.op("dve", lambda b=b: nc.vector.tensor_tensor(out=rt[b][:], in0=xt[b][:], in1=gt[:], op=ALU.mult), reads=[r_xt[b], r_gb], writes=[r_rt[b]])
        cx.op("pool", lambda b=b: nc.gpsimd.tensor_tensor(out=xt[b][:], in0=rt[b][:], in1=bt[:], op=ALU.add), reads=[r_rt[b], r_gb], writes=[r_xt[b]])
        cx.dma("sp", lambda st=st: out_dram[st * 128:(st + 1) * 128, :], lambda b=b: xt[b][:], reads=[r_xt[b]], key="d_xo%d" % b)


def transposes_tok2feat(nc, cx, og, r_og, ogT, r_ogT, identb, r_id, psT, r_psT):
    k = 0
    for st in range(8):
        for c4 in range(4):
            pi = k % 2; k += 1
            Tv = psT[pi][:].bitcast(BF16)
            for j in range(4):
                c = c4 * 4 + j
                cx.op("pe", lambda Tv=Tv, j=j, st=st, c=c: nc.tensor.transpose(out=Tv[:, j * 128:(j + 1) * 128], in_=og[:, st, c * 128:(c + 1) * 128], identity=identb[:]),
                      reads=[r_og[st], r_id], writes=[r_psT[pi]])
            if k % 2:
                cx.op("act", lambda Tv=Tv, c4=c4, st=st: nc.scalar.copy(out=ogT[:, c4 * 4:(c4 + 1) * 4, st * 128:(st + 1) * 128], in_=Tv[:, 0:512].rearrange("p (j t) -> p j t", j=4)),
                      reads=[r_psT[pi]], writes=[r_ogT[c4 * 4 + j] for j in range(4)])
            else:
                cx.op("dve", lambda Tv=Tv, c4=c4, st=st: nc.vector.tensor_copy(out=ogT[:, c4 * 4:(c4 + 1) * 4, st * 128:(st + 1) * 128], in_=Tv[:, 0:512].rearrange("p (j t) -> p j t", j=4)),
                      reads=[r_psT[pi]], writes=[r_ogT[c4 * 4 + j] for j in range(4)])


def build_B(nc, cx):
    def dram(name, shape, dt, kind):
        return nc.dram_tensor(name, shape, dt, kind=kind).ap()
    qT = dram("qT", [16, 128, S_LOC], BF16, "ExternalInput")
    KT = dram("KT", [4, 128, 8192], BF16, "ExternalInput")
    V = dram("V", [4, 8192, 128], BF16, "ExternalInput")
    gz = dram("gz", [S_LOC, 2048], F32, "ExternalInput")
    x = dram("x", [S_LOC, 2048], F32, "ExternalInput")
    w_out = dram("w_out", [2048, 2048], F32, "ExternalInput")
    ln_g = dram("ln_g", [1, 2048], F32, "ExternalInput")
    ln_b = dram("ln_b", [1, 2048], F32, "ExternalInput")
    ident = dram("ident", [128, 128], F32, "ExternalInput")
    h1 = dram("h1", [S_LOC, 2048], F32, "ExternalOutput")

    with ExitStack() as es0:
        def sb0(name, shape, dt):
            return es0.enter_context(nc.sbuf_tensor(name, shape, dt))
        def ps0(name, shape, dt=F32):
            return es0.enter_context(nc.psum_tensor(name, shape, dt))
        og = sb0("og", [128, 8, 2048], BF16); r_og = [Res("og%d" % i) for i in range(8)]
        identb = sb0("identb", [128, 128], BF16); r_id = Res("identb")
        cx.dma("pool", lambda: identb[:], lambda: ident[:, :], writes=[r_id], key="d_c")
        sbuf_ps = [ps0("sps%d" % i, [128, 1024]) for i in range(2)]; r_s = [Res("sps%d" % i) for i in range(2)]
        O = [ps0("ops%d" % i, [128, 512]) for i in range(4)]; r_O = [Res("ops%d" % i) for i in range(4)]
        with ExitStack() as es1:
            def sb(name, shape, dt):
                return es1.enter_context(nc.sbuf_tensor(name, shape, dt))
            qTs = sb("qTs", [128, 16, S_LOC], BF16); r_q = [Res("qTs%d" % i) for i in range(16)]
            KTs = [sb("KTs%d" % i, [128, 8192], BF16) for i in range(2)]; r_K = [Res("KTs%d" % i) for i in range(2)]
            Vs = [sb("Vs%d" % i, [128, 64, 129], BF16) for i in range(2)]; r_V = [Res("Vs%d" % i) for i in range(2)]
            PT = [sb("PT%d" % i, [128, 1024], BF16) for i in range(3)]; r_PT = [Res("PT%d" % i) for i in range(3)]
            gzt = [sb("gzt%d" % i, [128, 4, 128], F32) for i in range(2)]; r_gz = [Res("gzt%d" % i) for i in range(2)]
            rinv = [sb("rinv%d" % i, [128, 4], F32) for i in range(2)]; r_ri = [Res("rinv%d" % i) for i in range(2)]
            for i in range(2):
                cx.op("pool", lambda i=i: nc.gpsimd.memset(Vs[i][:, :, 128:129], 1.0), writes=[r_V[i]])
            def load_kv(h):
                b = h % 2
                for half in range(2):
                    cx.dma("sp", lambda: KTs[b][:, half * 4096:(half + 1) * 4096], lambda: KT[h, :, half * 4096:(half + 1) * 4096], writes=[r_K[b]], key="d_K%d_%d" % (b, half))
                    cx.dma("sp", lambda: Vs[b][:, half * 32:(half + 1) * 32, 0:128], lambda: V[h, half * 4096:(half + 1) * 4096, :].rearrange("(t p) d -> p t d", p=128), writes=[r_V[b]], key="d_V%d_%d" % (b, half))
            load_kv(0)
            for hd in range(16):
                cx.dma("act", lambda hd=hd: qTs[:, hd, :], lambda hd=hd: qT[hd, :, :], writes=[r_q[hd]], key="d_q%d" % hd)
            iters = [(kvh, g, qh, kp) for kvh in range(4) for g in range(4) for qh in range(2) for kp in range(32)]
            NI = len(iters)

            def emit_qk(n):
                kvh, g, qh, kp = iters[n]
                b = kvh % 2; head = kvh * 4 + g; si = n % 2
                if kp == 0:
                    gb = (n // 32) % 2
                    cx.dma("act", lambda gb=gb, head=head, qh=qh: gzt[gb][:], lambda head=head, qh=qh: gz[qh * 512:(qh + 1) * 512, head * 128:(head + 1) * 128].rearrange("(s p) e -> p s e", p=128), writes=[r_gz[gb]], key="d_gz%d" % gb)
                S = sbuf_ps[si]
                for j2 in range(2):
                    kt = kp * 2 + j2
                    cx.op("pe", lambda S=S, j2=j2, kt=kt, b=b, head=head, qh=qh: nc.tensor.matmul(S[:, j2 * 512:(j2 + 1) * 512], lhsT=KTs[b][:, kt * 128:(kt + 1) * 128], rhs=qTs[:, head, qh * 512:(qh + 1) * 512], start=True, stop=True),
                          reads=[r_K[b], r_q[head]], writes=[r_s[si]])

            def emit_rest(n):
                kvh, g, qh, kp = iters[n]
                b = kvh % 2; head = kvh * 4 + g; si = n % 2; pi = n % 3
                gb = (n // 32) % 2
                S = sbuf_ps[si]
                cx.op("act", lambda S=S, pi=pi: nc.scalar.activation(out=PT[pi][:], in_=S[:], func=AF.Exp, scale=SCALE), reads=[r_s[si]], writes=[r_PT[pi]])
                for j2 in range(2):
                    kt = kp * 2 + j2
                    for sub in range(4):
                        cx.op("pe", lambda sub=sub, pi=pi, j2=j2, kt=kt, b=b, kp=kp: nc.tensor.matmul(O[sub][:, 0:129], lhsT=PT[pi][:, j2 * 512 + sub * 128:j2 * 512 + (sub + 1) * 128], rhs=Vs[b][:, kt, :], start=(kp == 0 and j2 == 0), stop=(kp == 31 and j2 == 1)),
                              reads=[r_PT[pi], r_V[b]], writes=[r_O[sub]])
                if kp == 31:
                    for sub in range(4):
                        cx.op("dve", lambda sub=sub, gb=gb: nc.vector.reciprocal(out=rinv[gb][:, sub:sub + 1], in_=O[sub][:, 128:129]), reads=[r_O[sub]], writes=[r_ri[gb]])
                        cx.op("dve", lambda sub=sub, gb=gb, qh=qh, head=head: nc.vector.scalar_tensor_tensor(out=og[:, qh * 4 + sub, head * 128:(head + 1) * 128], in0=O[sub][:, 0:128], scalar=rinv[gb][:, sub:sub + 1], in1=gzt[gb][:, sub, :], op0=ALU.mult, op1=ALU.mult),
                              reads=[r_O[sub], r_ri[gb], r_gz[gb]], writes=[r_og[qh * 4 + sub]])
                if g == 0 and qh == 0 and kp == 1 and kvh + 1 < 4:
                    load_kv(kvh + 1)

            emit_qk(0)
            for n in range(NI):
                if n + 1 < NI:
                    emit_qk(n + 1)
                emit_rest(n)
        cx.barrier()
        with ExitStack() as es2:
            def sb(name, shape, dt):
                return es2.enter_context(nc.sbuf_tensor(name, shape, dt))
            ogT = sb("ogT", [128, 16, S_LOC], BF16); r_ogT = [Res("ogT%d" % i) for i in range(16)]
            transposes_tok2feat(nc, cx, og, r_og, ogT, r_ogT, identb, r_id, sbuf_ps, r_s)
            outproj_ln(nc, cx, es2, ogT, r_ogT, w_out, x, ln_g, ln_b, h1, O, r_O)
    cx.finish("sp")


NG = 16; NJ = 1024; NSEG = 16; SEGL = 64
SPB = 512 // SEGL
PI = math.pi

def build_D(nc, cx):
    def dram(name, shape, dt, kind):
        return nc.dram_tensor(name, shape, dt, kind=kind).ap()
    U = dram("U", [NG, 128, NJ], F32, "ExternalInput")
    lamre = dram("lamre", [128, NG], F32, "ExternalInput")
    lamim = dram("lamim", [128, NG], F32, "ExternalInput")
    logdt = dram("logdt", [128, NG], F32, "ExternalInput")
    Bre = dram("Bre", [128, NG, 16], F32, "ExternalInput")
    Bim = dram("Bim", [128, NG, 16], F32, "ExternalInput")
    Cre = dram("Cre", [128, NG, 16], F32, "ExternalInput")
    Cim = dram("Cim", [128, NG, 16], F32, "ExternalInput")
    dsk = dram("dsk", [128, NG], F32, "ExternalInput")
    maskF = dram("maskF", [128, 128], F32, "ExternalInput")
    maskB = dram("maskB", [128, 128], F32, "ExternalInput")
    ident = dram("ident", [128, 128], F32, "ExternalInput")
    Gy = dram("Gy", [NG, 128, NJ], F32, "ExternalOutput")

    with ExitStack() as es0:
        def sb0(name, shape, dt=F32):
            return es0.enter_context(nc.sbuf_tensor(name, shape, dt))
        def ps0(name, shape, dt=F32):
            return es0.enter_context(nc.psum_tensor(name, shape, dt))
        PS = [ps0("ps%d" % i, [128, 512]) for i in range(8)]; r_PS = [Res("ps%d" % i) for i in range(8)]
        W1T = sb0("W1T", [128, NG, 2, 128], BF16); r_W1T = Res("W1T")
        Tm = sb0("Tm", [128, NG, 128], BF16); r_Tm = Res("Tm")
        W3 = sb0("W3", [128, NG, 2, 128], BF16); r_W3 = Res("W3")
        a8 = sb0("a8", [128, 2, NG]); r_a8 = Res("a8")
        dskt = sb0("dskt", [128, NG]); r_dsk = Res("dskt")
        cx.dma("sp", lambda: dskt[:], lambda: dsk[:, :], writes=[r_dsk], key="d_c")

        with ExitStack() as es1:
            cnt = [0]
            def sb(shape, dt=F32):
                cnt[0] += 1
                return es1.enter_context(nc.sbuf_tensor("pp%d" % cnt[0], shape, dt))
            R = Res("prep")
            def dv(fn, eng="dve"):
                cx.op(eng, fn, reads=[R], writes=[R])
            lr = sb([128, NG]); li = sb([128, NG]); ld = sb([128, NG])
            bre = sb([128, NG, 16]); bim = sb([128, NG, 16]); cre = sb([128, NG, 16]); cim = sb([128, NG, 16])
            mF = sb([128, 128]); mB = sb([128, 128]); idf = sb([128, 128])
            for dst, src in [(lr, lamre), (li, lamim), (ld, logdt), (mF, maskF), (mB, maskB), (idf, ident)]:
                cx.dma("sp", lambda dst=dst: dst[:], lambda src=src: src[:, :], writes=[R], key="d_c")
            for dst, src in [(bre, Bre), (bim, Bim), (cre, Cre), (cim, Cim)]:
                cx.dma("sp", lambda dst=dst: dst[:], lambda src=src: src[:, :, :], writes=[R], key="d_c")
            dt_ = sb([128, NG]); er = sb([128, NG]); th = sb([128, NG]); mag = sb([128, NG])
            halfpi = sb([128, 1])
            dv(lambda: nc.vector.memset(halfpi[:], PI / 2))
            dv(lambda: nc.scalar.activation(out=dt_[:], in_=ld[:], func=AF.Exp), "act")
            dv(lambda: nc.vector.tensor_tensor(out=er[:], in0=lr[:], in1=dt_[:], op=ALU.mult))
            dv(lambda: nc.vector.tensor_tensor(out=th[:], in0=li[:], in1=dt_[:], op=ALU.mult))
            dv(lambda: nc.scalar.activation(out=mag[:], in_=er[:], func=AF.Exp), "act")
            ure = sb([128, NG]); uim = sb([128, NG]); t1 = sb([128, NG]); t2 = sb([128, NG])
            dv(lambda: nc.scalar.activation(out=uim[:], in_=th[:], func=AF.Sin, scale=0.125), "act")
            dv(lambda: nc.scalar.activation(out=ure[:], in_=th[:], func=AF.Sin, scale=-0.125, bias=halfpi[:, 0:1]), "act")
            for _ in range(3):
                dv(lambda: nc.vector.tensor_tensor(out=t1[:], in0=ure[:], in1=ure[:], op=ALU.mult))
                dv(lambda: nc.vector.tensor_tensor(out=t2[:], in0=uim[:], in1=uim[:], op=ALU.mult))
                dv(lambda: nc.vector.scalar_tensor_tensor(out=uim[:], in0=ure[:], scalar=2.0, in1=uim[:], op0=ALU.mult, op1=ALU.mult))
                dv(lambda: nc.vector.tensor_tensor(out=ure[:], in0=t1[:], in1=t2[:], op=ALU.subtract))
            are = sb([128, NG]); aim = sb([128, NG]); nre = sb([128, NG]); nim = sb([128, NG])
            dv(lambda: nc.vector.tensor_tensor(out=are[:], in0=ure[:], in1=mag[:], op=ALU.mult))
            dv(lambda: nc.vector.tensor_tensor(out=aim[:], in0=uim[:], in1=mag[:], op=ALU.mult))
            rmag = sb([128, NG])
            dv(lambda: nc.vector.reciprocal(out=rmag[:], in_=mag[:]))
            dv(lambda: nc.vector.tensor_tensor(out=nre[:], in0=ure[:], in1=rmag[:], op=ALU.mult))
            dv(lambda: nc.vector.scalar_tensor_tensor(out=nim[:], in0=uim[:], scalar=-1.0, in1=rmag[:], op0=ALU.mult, op1=ALU.mult))
            den = sb([128, NG]); nr = sb([128, NG]); cfr = sb([128, NG]); cfi = sb([128, NG])
            dv(lambda: nc.vector.tensor_tensor(out=den[:], in0=lr[:], in1=lr[:], op=ALU.mult))
            dv(lambda: nc.vector.tensor_tensor(out=t1[:], in0=li[:], in1=li[:], op=ALU.mult))
            dv(lambda: nc.vector.tensor_tensor(out=den[:], in0=den[:], in1=t1[:], op=ALU.add))
            dv(lambda: nc.vector.reciprocal(out=den[:], in_=den[:]))
            dv(lambda: nc.vector.tensor_scalar(out=nr[:], in0=are[:], scalar1=-1.0, scalar2=None, op0=ALU.add))
            dv(lambda: nc.vector.tensor_tensor(out=t1[:], in0=nr[:], in1=lr[:], op=ALU.mult))
            dv(lambda: nc.vector.tensor_tensor(out=t2[:], in0=aim[:], in1=li[:], op=ALU.mult))
            dv(lambda: nc.vector.tensor_tensor(out=t1[:], in0=t1[:], in1=t2[:], op=ALU.add))
            dv(lambda: nc.vector.tensor_tensor(out=cfr[:], in0=t1[:], in1=den[:], op=ALU.mult))
            dv(lambda: nc.vector.tensor_tensor(out=t1[:], in0=aim[:], in1=lr[:], op=ALU.mult))
            dv(lambda: nc.vector.tensor_tensor(out=t2[:], in0=nr[:], in1=li[:], op=ALU.mult))
            dv(lambda: nc.vector.tensor_tensor(out=t1[:], in0=t1[:], in1=t2[:], op=ALU.subtract))
            dv(lambda: nc.vector.tensor_tensor(out=cfi[:], in0=t1[:], in1=den[:], op=ALU.mult))

            def cmul(ore, oim, xre, xim, yre, yim, ta, tb):
                dv(lambda: nc.vector.tensor_tensor(out=ta(), in0=xre(), in1=yre(), op=ALU.mult))
                dv(lambda: nc.vector.tensor_tensor(out=tb(), in0=xim(), in1=yim(), op=ALU.mult))
                dv(lambda: nc.vector.tensor_tensor(out=ta(), in0=ta(), in1=tb(), op=ALU.subtract))
                dv(lambda: nc.vector.tensor_tensor(out=tb(), in0=xre(), in1=yim(), op=ALU.mult))
                dv(lambda: nc.vector.tensor_tensor(out=oim(), in0=xim(), in1=yre(), op=ALU.mult))
                dv(lambda: nc.vector.tensor_tensor(out=oim(), in0=oim(), in1=tb(), op=ALU.add))
                dv(lambda: nc.vector.tensor_copy(out=ore(), in_=ta()))
            bbr = sb([128, NG, 16]); bbi = sb([128, NG, 16]); ta3 = sb([128, NG, 16]); tb3 = sb([128, NG, 16])
            bc = lambda t: t[:, :].unsqueeze(2).to_broadcast([128, NG, 16])
            cmul(lambda: bbr[:], lambda: bbi[:], lambda: bre[:], lambda: bim[:], lambda: bc(cfr), lambda: bc(cfi), lambda: ta3[:], lambda: tb3[:])
            pw = sb([128, 2, NG, 9]); pn = sb([128, 2, NG, 9])
            tk = sb([128, NG, 4]); tk2 = sb([128, NG, 4])
            for tab, xr, xi in [(pw, are, aim), (pn, nre, nim)]:
                dv(lambda tab=tab: nc.vector.memset(tab[:, 0, :, 0:1], 1.0))
                dv(lambda tab=tab: nc.vector.memset(tab[:, 1, :, 0:1], 0.0))
                dv(lambda tab=tab, xr=xr: nc.vector.tensor_copy(out=tab[:, 0, :, 1], in_=xr[:]))
                dv(lambda tab=tab, xi=xi: nc.vector.tensor_copy(out=tab[:, 1, :, 1], in_=xi[:]))
                for (lo, n) in [(1, 1), (2, 2), (4, 4)]:
                    n_eff = min(n, 9 - (lo + 1))
                    cmul(lambda tab=tab, lo=lo, n_eff=n_eff: tab[:, 0, :, lo + 1:lo + 1 + n_eff], lambda tab=tab, lo=lo, n_eff=n_eff: tab[:, 1, :, lo + 1:lo + 1 + n_eff],
                         lambda tab=tab, n_eff=n_eff: tab[:, 0, :, 1:1 + n_eff], lambda tab=tab, n_eff=n_eff: tab[:, 1, :, 1:1 + n_eff],
                         lambda tab=tab, lo=lo, n_eff=n_eff: tab[:, 0, :, lo:lo + 1].to_broadcast([128, NG, n_eff]), lambda tab=tab, lo=lo, n_eff=n_eff: tab[:, 1, :, lo:lo + 1].to_broadcast([128, NG, n_eff]),
                         lambda n_eff=n_eff: tk[:, :, 0:n_eff], lambda n_eff=n_eff: tk2[:, :, 0:n_eff])
            dv(lambda: nc.vector.tensor_copy(out=a8[:, 0, :], in_=pw[:, 0, :, 8]))
            dv(lambda: nc.vector.tensor_copy(out=a8[:, 1, :], in_=pw[:, 1, :, 8]))
            cx.op("dve", lambda: nc.vector.tensor_copy(out=a8[:, 0, :], in_=pw[:, 0, :, 8]), reads=[R], writes=[r_a8])
            cx.op("dve", lambda: nc.vector.tensor_copy(out=a8[:, 1, :], in_=pw[:, 1, :, 8]), reads=[R], writes=[r_a8])
            E = sb([128, 2, NG, 8]); En = sb([128, 2, NG, 8]); Fo = sb([128, 2, NG, 8])
            for s in range(8):
                for ri in range(2):
                    dv(lambda s=s, ri=ri: nc.vector.tensor_copy(out=E[0:64, ri, :, s], in_=pw[0:64, ri, :, 7 - s]))
                    dv(lambda s=s, ri=ri: nc.vector.tensor_copy(out=E[64:128, ri, :, s], in_=pw[64:128, ri, :, s]))
                    dv(lambda s=s, ri=ri: nc.vector.tensor_copy(out=En[0:64, ri, :, s], in_=pn[0:64, ri, :, 7 - s]))
                    dv(lambda s=s, ri=ri: nc.vector.tensor_copy(out=En[64:128, ri, :, s], in_=pn[64:128, ri, :, s]))
                    dv(lambda s=s, ri=ri: nc.vector.tensor_copy(out=Fo[0:64, ri, :, s], in_=pw[0:64, ri, :, s + 1]))
                    dv(lambda s=s, ri=ri: nc.vector.tensor_copy(out=Fo[64:128, ri, :, s], in_=pw[64:128, ri, :, 8 - s]))
            big = [128, NG, 8, 16]
            W1f = sb([128, 2, NG, 128]); W3n = sb([128, 2, NG, 128]); W3f = sb([128, 2, NG, 128])
            tA = sb([128, NG, 128]); tB = sb([128, NG, 128])
            v4 = lambda t, ri: t[:, ri, :, :].rearrange("p g (s c) -> p g s c", c=16)
            t4 = lambda t: t[:].rearrange("p g (s c) -> p g s c", c=16)
            eb = lambda t, ri: t[:, ri, :, :].unsqueeze(3).to_broadcast(big)
            pb = lambda t: t[:, :, :].unsqueeze(2).to_broadcast(big)
            for (dstt, tab, pr, pi_) in [(W1f, E, bbr, bbi), (W3n, En, cre, cim), (W3f, Fo, cre, cim)]:
                cmul(lambda dstt=dstt: v4(dstt, 0), lambda dstt=dstt: v4(dstt, 1), lambda tab=tab: eb(tab, 0), lambda tab=tab: eb(tab, 1),
                     lambda pr=pr: pb(pr), lambda pi_=pi_: pb(pi_), lambda: t4(tA), lambda: t4(tB))
            dv(lambda: nc.vector.tensor_scalar(out=W3n[:, 1, :, :], in0=W3n[:, 1, :, :], scalar1=-1.0, scalar2=None, op0=ALU.mult))
            dv(lambda: nc.vector.tensor_scalar(out=W3f[:, 1, :, :], in0=W3f[:, 1, :, :], scalar1=-1.0, scalar2=None, op0=ALU.mult))
            cx.op("dve", lambda: nc.vector.tensor_copy(out=W3[:, :, 0, :], in_=W3f[:, 0, :, :]), reads=[R], writes=[r_W3])
            cx.op("dve", lambda: nc.vector.tensor_copy(out=W3[:, :, 1, :], in_=W3f[:, 1, :, :]), reads=[R], writes=[r_W3])
            tmpT = sb([128, 128])
            r_tmpT = Res("tmpT")
            for g in range(NG):
                for ri in range(2):
                    pi = (g * 2 + ri) % 4
                    cx.op("pe", lambda g=g, ri=ri, pi=pi: nc.tensor.transpose(out=PS[pi][:, 0:128], in_=W1f[:, ri, g, :], identity=idf[:]), reads=[R], writes=[r_PS[pi]])
                    cx.op("act", lambda g=g, ri=ri, pi=pi: nc.scalar.copy(out=W1T[:, g, ri, :], in_=PS[pi][:, 0:128]), reads=[r_PS[pi]], writes=[r_W1T])
                pf = 4 + (g % 2) * 2; pb_ = pf + 1
                for (pp, lo) in [(pf, 0), (pb_, 64)]:
                    for ri in range(2):
                        cx.op("pe", lambda g=g, ri=ri, pp=pp, lo=lo: nc.tensor.matmul(PS[pp][:, 0:128], lhsT=W1f[lo:lo + 64, ri, g, :], rhs=W3n[lo:lo + 64, ri, g, :], start=(ri == 0), stop=(ri == 1)),
                              reads=[R], writes=[r_PS[pp]])
                cx.op("dve", lambda pf=pf: nc.vector.tensor_tensor(out=tmpT[:], in0=PS[pf][:, 0:128], in1=mF[:], op=ALU.mult), reads=[r_PS[pf], R], writes=[r_tmpT])
                cx.op("pool", lambda pb_=pb_: nc.gpsimd.tensor_copy(out=tA[:, 0, :], in_=mB[:]), reads=[R], writes=[R]) if False else None
                cx.op("dve", lambda pb_=pb_: nc.vector.tensor_tensor(out=tA[:, 0, :], in0=PS[pb_][:, 0:128], in1=mB[:], op=ALU.mult), reads=[r_PS[pb_], R], writes=[R])
                cx.op("dve", lambda g=g: nc.vector.tensor_tensor(out=Tm[:, g, :], in0=tmpT[:], in1=tA[:, 0, :], op=ALU.add), reads=[r_tmpT, R], writes=[r_Tm])
        cx.barrier()

        with ExitStack() as es2:
            def sb(name, shape, dt=F32):
                return es2.enter_context(nc.sbuf_tensor(name, shape, dt))
            HG = 8
            RS = sb("RS", [128, 2, HG, NSEG + 1, SEGL])
            r_RSc = [[Res("RSf_re"), Res("RSf_im")], [Res("RSb_re"), Res("RSb_im")]]
            r_tt = [[Res("tt%d_%d" % (d, i)) for i in range(4)] for d in range(2)]
            Uf = sb("Uf", [128, HG, NJ]); r_Uf = [Res("Uf%d" % i) for i in range(HG)]
            Ub = sb("Ub", [128, HG, NJ], BF16); r_Ub = [Res("Ub%d" % i) for i in range(HG)]
            Sin = [sb("Sin%d" % i, [128, 2, NJ], BF16) for i in range(2)]; r_Sin = [Res("Sin%d" % i) for i in range(2)]
            Z = sb("Z", [128, 2, HG, NSEG]); r_Z = [Res("Zf"), Res("Zb")]
            tt = [[sb("tt%d_%d" % (d, i), [128, HG, NSEG + 1]) for i in range(4)] for d in range(2)]
            tz = [[sb("tz%d_%d" % (d, i), [128, HG]) for i in range(2)] for d in range(2)]
            tc_ = [[sb("tc%d_%d" % (d, i), [128, HG, SEGL]) for i in range(2)] for d in range(2)]
            yo = [sb("yo%d" % i, [128, 512]) for i in range(2)]; r_yo = [Res("yo%d" % i) for i in range(2)]
            yt = [sb("yt%d" % i, [128, 512]) for i in range(2)]; r_yt = [Res("yt%d" % i) for i in range(2)]
            for i in range(2):
                cx.op("pool", lambda i=i: nc.gpsimd.memset(Sin[i][:], 0.0), writes=[r_Sin[i]])
            V = nc.vector
            E_ = "dve"
            rre, rim = Res("RS_re"), Res("RS_im")
            rboth = [rre, rim]
            rT = [Res("tt%d" % i) for i in range(4)]
            T = tt[0]
            A, B = tz[0]
            CA, CB = tc_[0]
            def flat(rows, ri, gl):
                return RS[rows, ri, gl, 0:NSEG, :].rearrange("p s i -> p (s i)")
            for half in range(NG // HG):
                g0 = half * HG
                cx.op(E_, lambda: V.memset(RS[:, :, :, NSEG, :], 0.0), writes=rboth)
                cx.op(E_, lambda g0=g0: V.tensor_copy(out=RS[:, :, :, NSEG, 0], in_=a8[:, :, g0:g0 + HG]), reads=[r_a8], writes=rboth)
                for gl in range(HG):
                    g = g0 + gl
                    cx.dma("sp", lambda gl=gl, g=g: Uf[:, gl, :], lambda g=g: U[g, :, :], writes=[r_Uf[gl]], key="d_U%d" % gl)
                    cx.op("pool", lambda gl=gl: nc.gpsimd.tensor_copy(out=Ub[:, gl, :], in_=Uf[:, gl, :]), reads=[r_Uf[gl]], writes=[r_Ub[gl]])
                    for ri in range(2):
                        for blk in range(2):
                            pi = (ri * 2 + blk) % 4
                            cx.op("pe", lambda gl=gl, g=g, ri=ri, blk=blk, pi=pi: nc.tensor.matmul(PS[pi][:], lhsT=W1T[:, g, ri, :], rhs=Ub[:, gl, blk * 512:(blk + 1) * 512], start=True, stop=True),
                                  reads=[r_W1T, r_Ub[gl]], writes=[r_PS[pi]])
                            cx.op("act", lambda gl=gl, ri=ri, blk=blk, pi=pi: nc.scalar.copy(out=RS[0:64, ri, gl, blk * SPB:(blk + 1) * SPB, :], in_=PS[pi][0:64, :].rearrange("p (s i) -> p s i", i=SEGL)),
                                  reads=[r_PS[pi]], writes=[rboth[ri]])
                            cx.op("act", lambda gl=gl, ri=ri, blk=blk, pi=pi: nc.scalar.copy(out=flat(slice(64, 128), ri, gl)[:, (1 - blk) * 512:(2 - blk) * 512][:, ::-1], in_=PS[pi][64:128, :]),
                                  reads=[r_PS[pi]], writes=[rboth[ri]])
                ar_ = lambda g0=g0: a8[:, 0, g0:g0 + HG].unsqueeze(2).to_broadcast([128, HG, NSEG + 1])
                ai_ = lambda g0=g0: a8[:, 1, g0:g0 + HG].unsqueeze(2).to_broadcast([128, HG, NSEG + 1])
                for i in range(1, SEGL):
                    ip = i - 1
                    cx.op(E_, lambda ip=ip, ai_=ai_: V.tensor_tensor(out=T[1][:], in0=RS[:, 1, :, :, ip], in1=ai_(), op=ALU.mult), reads=[rim, r_a8], writes=[rT[1]])
                    cx.op(E_, lambda ip=ip, ar_=ar_: V.tensor_tensor(out=T[0][:], in0=RS[:, 0, :, :, ip], in1=ar_(), op=ALU.mult), reads=[rre, r_a8], writes=[rT[0]])
                    cx.op(E_, lambda ip=ip, ai_=ai_: V.tensor_tensor(out=T[2][:], in0=RS[:, 0, :, :, ip], in1=ai_(), op=ALU.mult), reads=[rre, r_a8], writes=[rT[2]])
                    cx.op(E_, lambda ip=ip, ar_=ar_: V.tensor_tensor(out=T[3][:], in0=RS[:, 1, :, :, ip], in1=ar_(), op=ALU.mult), reads=[rim, r_a8], writes=[rT[3]])
                    cx.op(E_, lambda: V.tensor_tensor(out=T[2][:], in0=T[2][:], in1=T[3][:], op=ALU.add), reads=[rT[2], rT[3]], writes=[rT[2]])
                    cx.op(E_, lambda: V.tensor_tensor(out=T[0][:], in0=T[0][:], in1=T[1][:], op=ALU.subtract), reads=[rT[0], rT[1]], writes=[rT[0]])
                    cx.op(E_, lambda i=i: V.tensor_tensor(out=RS[:, 0, :, :, i], in0=RS[:, 0, :, :, i], in1=T[0][:], op=ALU.add), reads=[rT[0], rre], writes=[rre])
                    cx.op(E_, lambda i=i: V.tensor_tensor(out=RS[:, 1, :, :, i], in0=RS[:, 1, :, :, i], in1=T[2][:], op=ALU.add), reads=[rT[2], rim], writes=[rim])
                last = SEGL - 1
                rr = rboth + [r_Z[0]]
                def o(fn):
                    cx.op(E_, fn, reads=rr, writes=rr)
                pr = lambda: RS[:, 0, :, NSEG, last]
                pim = lambda: RS[:, 1, :, NSEG, last]
                for m in range(NSEG):
                    o(lambda m=m: V.tensor_copy(out=Z[:, :, :, m], in_=RS[:, :, :, m, last]))
                    if m > 0:
                        mp = m - 1
                        o(lambda mp=mp: V.tensor_tensor(out=A[:], in0=Z[:, 0, :, mp], in1=pr(), op=ALU.mult))
                        o(lambda mp=mp: V.tensor_tensor(out=B[:], in0=Z[:, 1, :, mp], in1=pim(), op=ALU.mult))
                        o(lambda: V.tensor_tensor(out=A[:], in0=A[:], in1=B[:], op=ALU.subtract))
                        o(lambda m=m: V.tensor_tensor(out=Z[:, 0, :, m], in0=Z[:, 0, :, m], in1=A[:], op=ALU.add))
                        o(lambda mp=mp: V.tensor_tensor(out=A[:], in0=Z[:, 0, :, mp], in1=pim(), op=ALU.mult))
                        o(lambda mp=mp: V.tensor_tensor(out=B[:], in0=Z[:, 1, :, mp], in1=pr(), op=ALU.mult))
                        o(lambda: V.tensor_tensor(out=A[:], in0=A[:], in1=B[:], op=ALU.add))
                        o(lambda m=m: V.tensor_tensor(out=Z[:, 1, :, m], in0=Z[:, 1, :, m], in1=A[:], op=ALU.add))
                shp = [128, HG, SEGL]
                Pb = lambda ri: RS[:, ri, :, NSEG, :]
                for m in range(1, NSEG):
                    Zb = lambda ri, m=m: Z[:, ri, :, m - 1:m].to_broadcast(shp)
                    o(lambda Zb=Zb: V.tensor_tensor(out=CA[:], in0=Pb(0), in1=Zb(0), op=ALU.mult))
                    o(lambda Zb=Zb: V.tensor_tensor(out=CB[:], in0=Pb(1), in1=Zb(1), op=ALU.mult))
                    o(lambda: V.tensor_tensor(out=CA[:], in0=CA[:], in1=CB[:], op=ALU.subtract))
                    o(lambda m=m: V.tensor_tensor(out=RS[:, 0, :, m, :], in0=RS[:, 0, :, m, :], in1=CA[:], op=ALU.add))
                    o(lambda Zb=Zb: V.tensor_tensor(out=CA[:], in0=Pb(0), in1=Zb(1), op=ALU.mult))
                    o(lambda Zb=Zb: V.tensor_tensor(out=CB[:], in0=Pb(1), in1=Zb(0), op=ALU.mult))
                    o(lambda: V.tensor_tensor(out=CA[:], in0=CA[:], in1=CB[:], op=ALU.add))
                    o(lambda m=m: V.tensor_tensor(out=RS[:, 1, :, m, :], in0=RS[:, 1, :, m, :], in1=CA[:], op=ALU.add))
                for gl in range(HG):
                    g = g0 + gl
                    sbi = gl % 2
                    S_ = Sin[sbi]
                    for ri in range(2):
                        cx.op("pool", lambda S_=S_, ri=ri, gl=gl: nc.gpsimd.tensor_copy(out=S_[0:64, ri, 1:NJ], in_=flat(slice(0, 64), ri, gl)[:, 0:NJ - 1]), reads=rboth, writes=[r_Sin[sbi]])
                        cx.op("act", lambda S_=S_, ri=ri, gl=gl: nc.scalar.copy(out=S_[64:128, ri, 0:NJ - 1], in_=flat(slice(64, 128), ri, gl)[:, 0:NJ - 1][:, ::-1]), reads=rboth, writes=[r_Sin[sbi]])
                    for blk in range(2):
                        pi = 4 + (gl * 2 + blk) % 4
                        yb = (gl * 2 + blk) % 2
                        cs = slice(blk * 512, (blk + 1) * 512)
                        cx.op("pe", lambda g=g, gl=gl, cs=cs, pi=pi: nc.tensor.matmul(PS[pi][:], lhsT=Tm[:, g, :], rhs=Ub[:, gl, cs], start=True, stop=False), reads=[r_Tm, r_Ub[gl]], writes=[r_PS[pi]])
                        cx.op("pe", lambda g=g, S_=S_, cs=cs, pi=pi: nc.tensor.matmul(PS[pi][:], lhsT=W3[:, g, 0, :], rhs=S_[:, 0, cs], start=False, stop=False), reads=[r_W3, r_Sin[sbi]], writes=[r_PS[pi]])
                        cx.op("pe", lambda g=g, S_=S_, cs=cs, pi=pi: nc.tensor.matmul(PS[pi][:], lhsT=W3[:, g, 1, :], rhs=S_[:, 1, cs], start=False, stop=True), reads=[r_W3, r_Sin[sbi]], writes=[r_PS[pi]])
                        cx.op("dve", lambda g=g, gl=gl, cs=cs, pi=pi, yb=yb: nc.vector.scalar_tensor_tensor(out=yt[yb][:], in0=Uf[:, gl, cs], scalar=dskt[:, g:g + 1], in1=PS[pi][:], op0=ALU.mult, op1=ALU.add),
                              reads=[r_Uf[gl], r_dsk, r_PS[pi]], writes=[r_yt[yb]])
                        cx.op("act", lambda yb=yb: nc.scalar.activation(out=yo[yb][:], in_=yt[yb][:], func=AF.Gelu_apprx_tanh), reads=[r_yt[yb]], writes=[r_yo[yb]])
                        cx.dma("sp", lambda g=g, cs=cs: Gy[g, :, cs], lambda yb=yb: yo[yb][:], reads=[r_yo[yb]], key="d_yo%d" % yb)
    cx.finish("sp")


def host_inputs_D(inp, u_full, core):
    g0 = core * NG
    uu = u_full[:, g0 * 16:(g0 + NG) * 16].reshape(NJ, 8, NG, 16)
    U = np.ascontiguousarray(uu.transpose(2, 1, 3, 0).reshape(NG, 128, NJ))
    def qg(a):
        return np.ascontiguousarray(a[:, g0:g0 + NG, :].transpose(0, 2, 1).reshape(128, NG))
    lamre = qg(inp["lam_re"][0]); lamim = qg(inp["lam_im"][0])
    logdt = np.ascontiguousarray(np.broadcast_to(inp["log_dt"][0][:, None, g0:g0 + NG], (2, 64, NG)).reshape(128, NG))
    def bq(a):
        return np.ascontiguousarray(a[:, g0:g0 + NG].transpose(0, 2, 1, 3).reshape(128, NG, 16))
    def cq(a):
        return np.ascontiguousarray(a[:, g0:g0 + NG].transpose(0, 3, 1, 2).reshape(128, NG, 16))
    dsk = np.ascontiguousarray(np.broadcast_to(inp["d_skip"][0][g0 * 16:(g0 + NG) * 16].reshape(NG, 16).T[None], (8, 16, NG)).reshape(128, NG))
    s_idx = np.arange(128) // 16
    maskF = (s_idx[None, :] >= s_idx[:, None]).astype(np.float32)
    maskB = (s_idx[None, :] <= s_idx[:, None]).astype(np.float32)
    return {"U": U, "lamre": lamre, "lamim": lamim, "logdt": logdt, "Bre": bq(inp["b_re"][0]), "Bim": bq(inp["b_im"][0]),
            "Cre": cq(inp["c_re"][0]), "Cim": cq(inp["c_im"][0]), "dsk": dsk, "maskF": maskF, "maskB": maskB, "ident": np.eye(128, dtype=np.float32)}


def build_E(nc, cx):
    def dram(name, shape, dt, kind):
        return nc.dram_tensor(name, shape, dt, kind=kind).ap()
    gT = dram("gT", [2048, S_LOC], F32, "ExternalInput")
    gtok = dram("gtok", [S_LOC, 2048], F32, "ExternalInput")
    gz = dram("gz", [S_LOC, 2048], F32, "ExternalInput")
    x = dram("x", [S_LOC, 2048], F32, "ExternalInput")
    w_glu = dram("w_glu", [2048, 2048], F32, "ExternalInput")
    b_glu = dram("b_glu", [1, 2048], F32, "ExternalInput")
    w_out = dram("w_out", [2048, 2048], F32, "ExternalInput")
    ln_g = dram("ln_g", [1, 2048], F32, "ExternalInput")
    ln_b = dram("ln_b", [1, 2048], F32, "ExternalInput")
    ident = dram("ident", [128, 128], F32, "ExternalInput")
    out = dram("out", [S_LOC, 2048], F32, "ExternalOutput")
    with ExitStack() as es0:
        def sb0(name, shape, dt):
            return es0.enter_context(nc.sbuf_tensor(name, shape, dt))
        def ps0(name, shape, dt=F32):
            return es0.enter_context(nc.psum_tensor(name, shape, dt))
        og = sb0("og", [128, 8, 2048], BF16); r_og = [Res("og%d" % i) for i in range(8)]
        identb = sb0("identb", [128, 128], BF16); r_id = Res("identb")
        cx.dma("pool", lambda: identb[:], lambda: ident[:, :], writes=[r_id], key="d_c")
        psT = [ps0("sps%d" % i, [128, 1024]) for i in range(2)]; r_psT = [Res("sps%d" % i) for i in range(2)]
        O = [ps0("ops%d" % i, [128, 512]) for i in range(4)]; r_O = [Res("ops%d" % i) for i in range(4)]
        with ExitStack() as es1:
            def sb(name, shape, dt):
                return es1.enter_context(nc.sbuf_tensor(name, shape, dt))
            gTb = sb("gTb", [128, 16, S_LOC], BF16); r_gT = [Res("gTb%d" % c) for c in range(16)]
            wg = sb("wg", [128, 16, 2048], BF16); r_wg = [Res("wg%d" % i) for i in range(4)]
            bgt = sb("bgt", [128, 2048], F32); r_bg = Res("bgt")
            gk = [sb("gk%d" % i, [128, 2048], F32) for i in range(2)]; r_gk = [Res("gk%d" % i) for i in range(2)]
            zk = [sb("zk%d" % i, [128, 2048], F32) for i in range(2)]; r_zk = [Res("zk%d" % i) for i in range(2)]
            tg = [sb("tg%d" % i, [128, 512], F32) for i in range(3)]; r_tg = [Res("tg%d" % i) for i in range(3)]
            for c in range(16):
                cx.dma("pool", lambda c=c: gTb[:, c, :], lambda c=c: gT[c * 128:(c + 1) * 128, :], writes=[r_gT[c]], key="d_g%d" % c)
            wv = w_glu.rearrange("(c p) n -> p c n", p=128)
            for cb in range(4):
                cx.dma("pool", lambda cb=cb: wg[:, :, cb * 512:(cb + 1) * 512], lambda cb=cb: wv[:, :, cb * 512:(cb + 1) * 512], writes=[r_wg[cb]], key="d_wg%d" % cb)
            cx.dma("sp", lambda: bgt[:], lambda: b_glu[0:1, :].to_broadcast([128, 2048]), writes=[r_bg], key="d_c2")
            it = 0
            for st in range(8):
                b = st % 2
                cx.dma("sp", lambda st=st, b=b: gk[b][:], lambda st=st: gtok[st * 128:(st + 1) * 128, :], writes=[r_gk[b]], key="d_gk%d" % b)
                cx.dma("act", lambda st=st, b=b: zk[b][:], lambda st=st: gz[st * 128:(st + 1) * 128, :], writes=[r_zk[b]], key="d_zk%d" % b)
                for cb in range(4):
                    P = O[cb]; ti = it % 3; it += 1
                    cs = slice(cb * 512, (cb + 1) * 512)
                    for c in range(16):
                        cx.op("pe", lambda c=c, P=P, st=st, cs=cs: nc.tensor.matmul(P[:], lhsT=gTb[:, c, st * 128:(st + 1) * 128], rhs=wg[:, c, cs], start=(c == 0), stop=(c == 15)),
                              reads=[r_gT[c], r_wg[cb]], writes=[r_O[cb]])
                    cx.op("dve", lambda P=P, ti=ti, cs=cs: nc.vector.tensor_tensor(out=tg[ti][:], in0=P[:], in1=bgt[:, cs], op=ALU.add), reads=[r_O[cb], r_bg], writes=[r_tg[ti]])
                    cx.op("act", lambda ti=ti: nc.scalar.activation(out=tg[ti][:], in_=tg[ti][:], func=AF.Sigmoid), reads=[r_tg[ti]], writes=[r_tg[ti]])
                    cx.op("dve", lambda ti=ti, b=b, cs=cs: nc.vector.tensor_tensor(out=tg[ti][:], in0=tg[ti][:], in1=gk[b][:, cs], op=ALU.mult), reads=[r_tg[ti], r_gk[b]], writes=[r_tg[ti]])
                    cx.op("pool", lambda ti=ti, b=b, cs=cs, st=st: nc.gpsimd.tensor_tensor(out=og[:, st, cs], in0=tg[ti][:], in1=zk[b][:, cs], op=ALU.mult), reads=[r_tg[ti], r_zk[b]], writes=[r_og[st]])
        cx.barrier()
        with ExitStack() as es2:
            ogT = es2.enter_context(nc.sbuf_tensor("ogT", [128, 16, S_LOC], BF16)); r_ogT = [Res("ogT%d" % i) for i in range(16)]
            transposes_tok2feat(nc, cx, og, r_og, ogT, r_ogT, identb, r_id, psT, r_psT)
            outproj_ln(nc, cx, es2, ogT, r_ogT, w_out, x, ln_g, ln_b, out, O, r_O)
    cx.finish("sp")


_PROGS = {}

def _prog(name, fn):
    if name not in _PROGS:
        _PROGS[name] = build2(fn)
    return _PROGS[name]


def kernel(x, ln_g, ln_b, w_in_attn, q_norm_g, k_norm_g, w_out_attn, w_in_ssm, lam_re, lam_im, log_dt,
           b_re, b_im, c_re, c_im, d_skip, w_glu, b_glu, w_out_ssm):
    NC = 8
    cores = list(range(NC))
    f32 = np.float32
    x0 = np.asarray(x, dtype=f32)[0]
    ident = np.eye(128, dtype=f32)
    inp = {"lam_re": np.asarray(lam_re, f32), "lam_im": np.asarray(lam_im, f32), "log_dt": np.asarray(log_dt, f32),
           "b_re": np.asarray(b_re, f32), "b_im": np.asarray(b_im, f32), "c_re": np.asarray(c_re, f32), "c_im": np.asarray(c_im, f32),
           "d_skip": np.asarray(d_skip, f32)}
    xs = [np.ascontiguousarray(x0[c * 1024:(c + 1) * 1024]) for c in cores]
    ims = []
    for c in cores:
        C, Sn, Sp = rope_tables(np.arange(c * 1024, (c + 1) * 1024))
        ims.append({"xT": np.ascontiguousarray(xs[c].T), "w_in": np.asarray(w_in_attn, f32)[0], "gq": np.asarray(q_norm_g, f32)[0][None, :],
                    "gk": np.asarray(k_norm_g, f32)[0][None, :], "ctab": C, "sntab": Sn, "sptab": Sp})
    rA = run_bass_kernel_spmd(_prog("A", lambda nc, cx: build_proj(nc, cx, "attn")), ims, core_ids=cores).results
    k_all = np.concatenate([np.asarray(r["k_o"]) for r in rA], axis=0)
    v_all = np.concatenate([np.asarray(r["v_o"]) for r in rA], axis=0)
    KT = np.ascontiguousarray(k_all.reshape(8192, 4, 128).transpose(1, 2, 0))
    V = np.ascontiguousarray(v_all.reshape(8192, 4, 128).transpose(1, 0, 2))
    ims = []
    for c in cores:
        q = np.asarray(rA[c]["q_o"])
        ims.append({"qT": np.ascontiguousarray(q.reshape(1024, 16, 128).transpose(1, 2, 0)), "KT": KT, "V": V,
                    "gz": np.asarray(rA[c]["gz_o"]), "x": xs[c], "w_out": np.asarray(w_out_attn, f32)[0],
                    "ln_g": np.asarray(ln_g, f32)[0:1], "ln_b": np.asarray(ln_b, f32)[0:1], "ident": ident})
    rB = run_bass_kernel_spmd(_prog("B", build_B), ims, core_ids=cores).results
    h1 = [np.asarray(r["h1"]) for r in rB]
    dummy = np.zeros((1, 128), f32)
    ims = []
    for c in cores:
        C, Sn, Sp = rope_tables(np.arange(c * 1024, (c + 1) * 1024))
        ims.append({"xT": np.ascontiguousarray(h1[c].T), "w_in": np.asarray(w_in_ssm, f32)[0], "gq": dummy, "gk": dummy, "ctab": C, "sntab": Sn, "sptab": Sp})
    rC = run_bass_kernel_spmd(_prog("C", lambda nc, cx: build_proj(nc, cx, "ssm")), ims, core_ids=cores).results
    u_full = np.concatenate([np.asarray(r["u_o"]) for r in rC], axis=0)
    ims = [host_inputs_D(inp, u_full, c) for c in cores]
    rD = run_bass_kernel_spmd(_prog("D", build_D), ims, core_ids=cores).results
    g_full = np.concatenate([np.asarray(r["Gy"]).reshape(NG, 8, 16, NJ).transpose(3, 1, 0, 2).reshape(8192, NG * 16) for r in rD], axis=1)
    ims = []
    for c in cores:
        gt = np.ascontiguousarray(g_full[c * 1024:(c + 1) * 1024])
        ims.append({"gT": np.ascontiguousarray(gt.T), "gtok": gt, "gz": np.asarray(rC[c]["gz_o"]), "x": h1[c],
                    "w_glu": np.asarray(w_glu, f32)[0], "b_glu": np.asarray(b_glu, f32)[0:1], "w_out": np.asarray(w_out_ssm, f32)[0],
                    "ln_g": np.asarray(ln_g, f32)[1:2], "ln_b": np.asarray(ln_b, f32)[1:2], "ident": ident})
    rE = run_bass_kernel_spmd(_prog("E", build_E), ims, core_ids=cores).results
    out = np.concatenate([np.asarray(r["out"]) for r in rE], axis=0)
    return out[None].astype(np.float32)
```
